# Optimizing a Trainium2 kernel written in Bass

```python
import math
import jax, jax.numpy as jnp
from jax import lax
import numpy as np

D_MODEL = 4096
BATCH = 2
SEQ = 8192
DEPTH = 1

CTX_LEN = 256
GRID_W = 64
MIX_W = D_MODEL
FOURIER_HEADS = 4
FOURIER_W = MIX_W // 2
FOURIER_HEAD_DIM = FOURIER_W // FOURIER_HEADS
S5_W = MIX_W - FOURIER_W
S5_GROUP = 16
S5_GROUPS = S5_W // S5_GROUP
S5_STATE = 64
FFN_HIDDEN = -(-8 * D_MODEL // (3 * 256)) * 256
N_MOD = 6
EPS = 1e-6
DT_MIN = 1e-3
DT_MAX = 1e-1

kernel_name = "fnet_s5_hybrid_prefix_dit_block"


def rms_norm(x, g):
    xf = x.astype(jnp.float32)
    y = xf * lax.rsqrt(jnp.mean(xf * xf, axis=-1, keepdims=True) + EPS)
    return (y * g.astype(jnp.float32)).astype(x.dtype)


def modulate(h, shift, scale):
    return h * (1 + scale) + shift


def fourier_mixer(u, w_f):
    bsz, length, _ = u.shape
    uh = u.reshape(bsz, length, FOURIER_HEADS, FOURIER_HEAD_DIM).astype(jnp.float32)
    f = jnp.fft.fftn(uh, axes=(1, 3), norm="ortho").real.astype(u.dtype)
    y = jnp.einsum('blhd,hde->blhe', f, w_f)
    return y.reshape(bsz, length, FOURIER_W)


def _ssm_combine(e1, e2):
    a1, b1 = e1
    a2, b2 = e2
    return a1 * a2, a2 * b1 + b2


def s5_scan(u, h0, lam_re, lam_im, log_dt, b_re, b_im, c_re, c_im):
    f32 = jnp.float32
    dt = jnp.exp(log_dt.astype(f32))[:, None]
    lam = lax.complex(jnp.minimum(lam_re.astype(f32), -1e-4), lam_im.astype(f32))
    lam_dt = lam * dt
    lam_bar = jnp.exp(lam_dt)
    b_bar = ((lam_bar - 1) / lam)[..., None] * lax.complex(b_re.astype(f32), b_im.astype(f32))
    c_mat = lax.complex(c_re.astype(f32), c_im.astype(f32))
    steps = jnp.arange(1, GRID_W + 1, dtype=f32)[:, None, None]
    carry_decay = jnp.exp(lam_dt[None] * steps)
    bsz, length, n_groups, group_w = u.shape
    rows = length // GRID_W
    u_rows = jnp.moveaxis(u.reshape(bsz, rows, GRID_W, n_groups, group_w), 1, 0)

    def row_step(h, u_r):
        bu = jnp.einsum('gph,bwgh->bwgp', b_bar, u_r.astype(jnp.complex64))
        a = jnp.broadcast_to(lam_bar, bu.shape)
        _, hs = lax.associative_scan(_ssm_combine, (a, bu), axis=1)
        hs = hs + carry_decay[None] * h[:, None]
        y = jnp.einsum('ghp,bwgp->bwgh', c_mat, hs).real
        return hs[:, -1], y

    h_last, ys = lax.scan(row_step, h0, u_rows)
    return jnp.moveaxis(ys, 0, 1).reshape(bsz, length, n_groups, group_w), h_last


def s5_bidir(u, h0_f, h0_b, lam_re, lam_im, log_dt, b_re, b_im, c_re, c_im, d_skip):
    bsz, length, _ = u.shape
    uf = u.astype(jnp.float32)
    ug = uf.reshape(bsz, length, S5_GROUPS, S5_GROUP)
    y_f, h_f = s5_scan(ug, h0_f, lam_re[0], lam_im[0], log_dt[0], b_re[0], b_im[0], c_re[0], c_im[0])
    y_b, h_b = s5_scan(jnp.flip(ug, 1), h0_b, lam_re[1], lam_im[1], log_dt[1], b_re[1], b_im[1], c_re[1], c_im[1])
    y = (y_f + jnp.flip(y_b, 1)).reshape(bsz, length, S5_W) + d_skip.astype(jnp.float32) * uf
    return y.astype(u.dtype), h_f, h_b


def s5_glu(y, w_a, b_a, w_b, b_b):
    g = jax.nn.gelu(y)
    return (g @ w_a + b_a) * jax.nn.sigmoid(g @ w_b + b_b)


def swiglu(h, w_gate, w_up, w_down):
    return (jax.nn.silu(h @ w_gate) * (h @ w_up)) @ w_down


def setup_inputs(seed: int = 0) -> dict:
    key = jax.random.key(seed)
    ks = jax.random.split(key, 32)
    f32 = jnp.float32
    nrm = lambda k, s, std: jax.random.normal(k, s, f32) * std
    G, P, H = S5_GROUPS, S5_STATE, S5_GROUP
    ada_std = 0.5 * D_MODEL ** -0.5
    lam_im = math.pi * jnp.arange(P, dtype=f32)[None, None, None, :] + nrm(ks[10], (DEPTH, 2, G, P), 0.01)
    return {
        "x": nrm(ks[0], (BATCH, SEQ, D_MODEL), 1.0),
        "c": nrm(ks[1], (BATCH, D_MODEL), 1.0),
        "ctx": nrm(ks[2], (BATCH, CTX_LEN, D_MODEL), 1.0),
        "c_ctx": nrm(ks[3], (D_MODEL,), 1.0),
        "ada_w": nrm(ks[4], (DEPTH, D_MODEL, N_MOD * D_MODEL), ada_std),
        "ada_b": nrm(ks[5], (DEPTH, N_MOD * D_MODEL), 0.01),
        "norm1_g": 1.0 + nrm(ks[6], (DEPTH, D_MODEL), 0.01),
        "norm2_g": 1.0 + nrm(ks[7], (DEPTH, D_MODEL), 0.01),
        "w_in": nrm(ks[8], (DEPTH, D_MODEL, MIX_W), D_MODEL ** -0.5),
        "w_out": nrm(ks[9], (DEPTH, MIX_W, D_MODEL), MIX_W ** -0.5),
        "fourier_w": nrm(ks[11], (DEPTH, FOURIER_HEADS, FOURIER_HEAD_DIM, FOURIER_HEAD_DIM), FOURIER_HEAD_DIM ** -0.5),
        "s5_lam_re": -0.5 + nrm(ks[12], (DEPTH, 2, G, P), 0.01),
        "s5_lam_im": lam_im,
        "s5_log_dt": jax.random.uniform(ks[13], (DEPTH, 2, G), f32, math.log(DT_MIN), math.log(DT_MAX)),
        "s5_b_re": nrm(ks[14], (DEPTH, 2, G, P, H), (2.0 * H) ** -0.5),
        "s5_b_im": nrm(ks[15], (DEPTH, 2, G, P, H), (2.0 * H) ** -0.5),
        "s5_c_re": nrm(ks[16], (DEPTH, 2, G, H, P), (2.0 * P) ** -0.5),
        "s5_c_im": nrm(ks[17], (DEPTH, 2, G, H, P), (2.0 * P) ** -0.5),
        "s5_d": nrm(ks[18], (DEPTH, S5_W), 1.0),
        "glu_w_a": nrm(ks[19], (DEPTH, S5_W, S5_W), S5_W ** -0.5),
        "glu_b_a": nrm(ks[20], (DEPTH, S5_W), 0.01),
        "glu_w_b": nrm(ks[21], (DEPTH, S5_W, S5_W), S5_W ** -0.5),
        "glu_b_b": nrm(ks[22], (DEPTH, S5_W), 0.01),
        "ffn_w_gate": nrm(ks[23], (DEPTH, D_MODEL, FFN_HIDDEN), D_MODEL ** -0.5),
        "ffn_w_up": nrm(ks[24], (DEPTH, D_MODEL, FFN_HIDDEN), D_MODEL ** -0.5),
        "ffn_w_down": nrm(ks[25], (DEPTH, FFN_HIDDEN, D_MODEL), FFN_HIDDEN ** -0.5),
        "final_g": 1.0 + nrm(ks[26], (D_MODEL,), 0.01),
    }


def reference(x, c, ctx, c_ctx, ada_w, ada_b, norm1_g, norm2_g, w_in, w_out, fourier_w,
              s5_lam_re, s5_lam_im, s5_log_dt, s5_b_re, s5_b_im, s5_c_re, s5_c_im, s5_d,
              glu_w_a, glu_b_a, glu_w_b, glu_b_b, ffn_w_gate, ffn_w_up, ffn_w_down, final_g):
    bsz = x.shape[0]
    h_zero = jnp.zeros((bsz, S5_GROUPS, S5_STATE), jnp.complex64)
    for layer in range(DEPTH):
        last = layer == DEPTH - 1
        s5_p = (s5_lam_re[layer], s5_lam_im[layer], s5_log_dt[layer], s5_b_re[layer], s5_b_im[layer],
                s5_c_re[layer], s5_c_im[layer], s5_d[layer])
        glu_p = (glu_w_a[layer], glu_b_a[layer], glu_w_b[layer], glu_b_b[layer])
        ffn_p = (ffn_w_gate[layer], ffn_w_up[layer], ffn_w_down[layer])
        mod = (jax.nn.silu(c) @ ada_w[layer] + ada_b[layer]).reshape(bsz, 1, N_MOD, D_MODEL)
        mod_c = (jax.nn.silu(c_ctx) @ ada_w[layer] + ada_b[layer]).reshape(1, 1, N_MOD, D_MODEL)
        sh1, sc1, g1, sh2, sc2, g2 = (mod[:, :, i] for i in range(N_MOD))
        csh1, csc1, cg1, csh2, csc2, cg2 = (mod_c[:, :, i] for i in range(N_MOD))

        hc = modulate(rms_norm(ctx, norm1_g[layer]), csh1, csc1)
        yc_s, hf_ctx, hb_ctx = s5_bidir(hc @ w_in[layer][:, FOURIER_W:], h_zero, h_zero, *s5_p)

        h = modulate(rms_norm(x, norm1_g[layer]), sh1, sc1)
        z = h @ w_in[layer]
        y_four = fourier_mixer(z[..., :FOURIER_W], fourier_w[layer])
        y_s, _, _ = s5_bidir(z[..., FOURIER_W:], hf_ctx, hb_ctx, *s5_p)
        y_s = s5_glu(y_s, *glu_p)
        x = x + g1 * (jnp.concatenate([y_four, y_s], axis=-1) @ w_out[layer])
        x = x + g2 * swiglu(modulate(rms_norm(x, norm2_g[layer]), sh2, sc2), *ffn_p)

        if not last:
            yc_four = fourier_mixer(hc @ w_in[layer][:, :FOURIER_W], fourier_w[layer])
            yc = jnp.concatenate([yc_four, s5_glu(yc_s, *glu_p)], axis=-1) @ w_out[layer]
            ctx = ctx + cg1 * yc
            ctx = ctx + cg2 * swiglu(modulate(rms_norm(ctx, norm2_g[layer]), csh2, csc2), *ffn_p)
    return rms_norm(x, final_g)
```

```python
import math
import numpy as np
import ml_dtypes
from contextlib import ExitStack
import concourse.bass as bass
import concourse.mybir as mybir
from concourse.bass_utils import run_bass_kernel_spmd

F32 = mybir.dt.float32
BF16 = mybir.dt.bfloat16
ALU = mybir.AluOpType
AF = mybir.ActivationFunctionType
AX = mybir.AxisListType
NPOOL = 24
EPS = 1e-6
MAGIC = 12582912.0
TWO_PI = 2.0 * math.pi


class Buf:
    __slots__ = ("W", "R", "war", "mode")

    def __init__(self):
        self.W = []
        self.R = []
        self.war = []
        self.mode = "w"


def _prune(lst):
    if len(lst) > 48:
        best = {}
        for k, v in lst:
            if best.get(k, 0) < v:
                best[k] = v
        lst[:] = list(best.items())


class Ring:
    def __init__(self, items):
        self.items = items
        self.i = 0

    def get(self):
        it = self.items[self.i]
        self.i = (self.i + 1) % len(self.items)
        return it


class Prog:
    ENGS = ("tensor", "vector", "scalar", "gpsimd", "sync")

    def __init__(self, nc, es):
        self.nc = nc
        self.es = es
        self.sems = {}
        self.engs = {}
        for name in self.ENGS:
            self.sems["s_" + name] = es.enter_context(nc.semaphore("s_" + name))
            self.engs[name] = dict(key="s_" + name, cnt=0, ops=[], seen={})
        self.pool = {}
        self.pnext = {}
        for q in ("sync", "gpsimd", "scalar"):
            self.pool[q] = []
            for i in range(NPOOL):
                key = f"d_{q}_{i}"
                self.sems[key] = es.enter_context(nc.semaphore(key))
                self.pool[q].append(dict(key=key, val=0))
            self.pnext[q] = 0
        self.uid = 0

    def name(self, base):
        self.uid += 1
        return f"{base}_{self.uid}"

    def _deps(self, E, ename, reads, writes, extra=(), strict=True):
        need = {}

        def add(tok):
            k, v = tok
            if ename == "tensor" and k == "s_tensor":
                return
            if need.get(k, 0) < v:
                need[k] = v

        for b in reads:
            for t in b.W:
                add(t)
        for b in writes:
            if b.mode == "r":
                for t in b.R:
                    add(t)
                for t in b.W:
                    add(t)
            else:
                for t in b.war:
                    add(t)
                if strict:
                    for t in b.W:
                        add(t)
        for t in extra:
            add(t)
        waits = []
        for k, v in need.items():
            if E["seen"].get(k, 0) >= v:
                continue
            E["seen"][k] = v
            waits.append((k, v))
        return waits

    @staticmethod
    def _mark(tok, reads, writes, strict=True):
        for b in reads:
            if b.mode == "w":
                b.mode = "r"
                b.R = [tok]
            else:
                b.R.append(tok)
                _prune(b.R)
        for b in writes:
            if b.mode == "r":
                b.war = [tok] if strict else b.R
                b.R = []
                b.W = [tok]
                b.mode = "w"
            elif strict:
                b.W = [tok]
                b.war = [tok]
            else:
                b.W.append(tok)
                _prune(b.W)

    def op(self, eng, fn, reads=(), writes=(), strict=True):
        E = self.engs[eng]
        waits = self._deps(E, eng, reads, writes, strict=strict)
        E["cnt"] += 1
        tok = (E["key"], E["cnt"])
        self._mark(tok, reads, writes, strict)
        E["ops"].append((waits, fn, (E["key"], 1)))
        return tok

    def I(self, eng, method, reads, writes, *args, strict=True, **kw):
        return self.op(eng, lambda e: getattr(e, method)(*args, **kw), reads, writes, strict)

    def mm(self, out, pairs, reads, writes, start=True, stop=True, strict=True):
        pairs = list(pairs)

        def fn(e):
            n = len(pairs)
            ins = None
            for i, (l, r) in enumerate(pairs):
                ins = e.matmul(out, l, r, start=(start and i == 0), stop=(stop and i == n - 1))
            return ins
        return self.op("tensor", fn, reads, writes, strict)

    def dma(self, q, out, in_, reads=(), writes=(), strict=True, **kw):
        E = self.engs[q]
        slot = self.pool[q][self.pnext[q]]
        self.pnext[q] = (self.pnext[q] + 1) % NPOOL
        extra = [(slot["key"], slot["val"])] if slot["val"] else []
        waits = self._deps(E, q, reads, writes, extra, strict)
        slot["val"] += 16
        tok = (slot["key"], slot["val"])
        self._mark(tok, reads, writes, strict)
        E["ops"].append((waits, lambda e: e.dma_start(out=out, in_=in_, **kw), (slot["key"], 16)))
        return tok

    def all_tokens(self):
        toks = []
        for q in self.pool:
            for s in self.pool[q]:
                if s["val"]:
                    toks.append((s["key"], s["val"]))
        for name in ("tensor", "vector", "scalar", "gpsimd"):
            E = self.engs[name]
            if E["cnt"]:
                toks.append((E["key"], E["cnt"]))
        return toks

    def barrier(self):
        toks = self.all_tokens()
        for name in self.ENGS:
            E = self.engs[name]
            waits = []
            for k, v in toks:
                if name == "tensor" and k == "s_tensor":
                    continue
                if E["seen"].get(k, 0) >= v:
                    continue
                E["seen"][k] = v
                waits.append((k, v))
            if waits:
                E["ops"].append((waits, None, None))

    def emit(self):
        nc = self.nc
        sems = self.sems
        final = self.all_tokens()

        def mk(name, is_last):
            ops = self.engs[name]["ops"]

            def body(e):
                for waits, fn, inc in ops:
                    for k, v in waits:
                        e.wait_ge(sems[k], v)
                    if fn is not None:
                        fn(e).then_inc(sems[inc[0]], inc[1])
                if is_last:
                    for k, v in final:
                        e.wait_ge(sems[k], v)
            return body

        with nc.Block() as block:
            block.tensor(mk("tensor", False))
            block.vector(mk("vector", False))
            block.scalar(mk("scalar", False))
            block.gpsimd(mk("gpsimd", False))
            block.sync(mk("sync", True))


class Phase:
    def __init__(self, P):
        self.P = P
        self.es = ExitStack()

    def __enter__(self):
        self.es.__enter__()
        return self

    def __exit__(self, *a):
        self.P.barrier()
        return self.es.__exit__(*a)

    def sb(self, base, shape, dtype):
        t = self.es.enter_context(self.P.nc.sbuf_tensor(self.P.name(base), list(shape), dtype))
        return t, Buf()

    def ps(self, base, shape, dtype=F32):
        t = self.es.enter_context(self.P.nc.psum_tensor(self.P.name(base), list(shape), dtype))
        return t, Buf()

    def ring(self, base, n, shape, dtype, psum=False):
        return Ring([(self.ps if psum else self.sb)(base, shape, dtype) for _ in range(n)])


class Cfg:
    def __init__(self, D=4096, FH=11008):
        self.D = D
        self.KC = D // 128
        self.FW = D // 2
        self.NH = 4
        self.HD = self.FW // 4
        self.HC = self.HD // 128
        self.FCH = self.FW // 128
        self.SW = D // 2
        self.G = self.SW // 16
        self.GT = self.G // 8
        self.FH = FH
        self.FC = FH // 128
        self.NMOD = 6
        self.L = 8192
        self.CTX = 256
        self.NK = 2048
        self.NB = 16
        self.RP = 132


def bf16(a):
    return np.asarray(a, dtype=np.float32).astype(ml_dtypes.bfloat16)


class T_:
    pass


def row_bcast(ap, n):
    dims = [list(d) for d in ap.ap]
    return bass.AP(ap.tensor, ap.offset, [[0, n]] + dims[1:])


def declare(nc, cfg, dbg):
    T = T_()
    D, KC, FW, SW, FH, G = cfg.D, cfg.KC, cfg.FW, cfg.SW, cfg.FH, cfg.G

    def inp(name, shape, dt=F32):
        setattr(T, name, nc.dram_tensor(name, list(shape), dt, kind="ExternalInput").ap())

    def scr(name, shape, dt, ext_in=False):
        kind = "Internal"
        if dbg:
            kind = "ExternalInput" if ext_in else "ExternalOutput"
        setattr(T, name, nc.dram_tensor(name, list(shape), dt, kind=kind).ap())

    inp("xb", [cfg.L, D])
    inp("xctx", [cfg.CTX, D])
    inp("xown", [cfg.NK, D])
    inp("cvec", [2, D])
    inp("ada_w", [D, 6 * D])
    inp("ada_b", [1, 6 * D])
    inp("norm1_g", [1, D])
    inp("norm2_g", [1, D])
    inp("final_g", [1, D])
    inp("w_in", [D, D])
    inp("w_out", [D, D])
    inp("fourier_w", [FW, cfg.HD])
    inp("lam_re", [2 * G, 64])
    inp("lam_im", [2 * G, 64])
    inp("log_dt", [1, 2 * G])
    inp("b_re", [2 * G * 64, 16])
    inp("b_im", [2 * G * 64, 16])
    inp("c_re", [2 * G * 16, 64])
    inp("c_im", [2 * G * 16, 64])
    inp("s5_d", [1, SW])
    inp("glu_w_a", [SW, SW])
    inp("glu_b_a", [1, SW])
    inp("glu_w_b", [SW, SW])
    inp("glu_b_b", [1, SW])
    inp("ffn_w_gate", [D, FH])
    inp("ffn_w_up", [D, FH])
    inp("ffn_w_down", [FH, D])
    inp("cdft", [cfg.HD, 2 * cfg.HD])
    inp("tdft", [4, 64, 128, 2, 512], BF16)
    inp("identb", [128, 128], BF16)
    inp("s5c", [128, S5C_N])
    T.out = nc.dram_tensor("out_own", [cfg.NK, D], F32, kind="ExternalOutput").ap()
    scr("modscr", [2, 6 * D], F32)
    scr("hTs", [cfg.NB, 128, KC, 512], BF16)
    scr("hTc", [128, KC, 256], BF16)
    scr("zF", [cfg.FCH, 128, cfg.L], BF16)
    scr("Dscr", [SW, 64, cfg.RP], BF16)
    scr("ABs", [64, 128, 2, FW], BF16)
    scr("ycat", [KC, 128, cfg.NK], BF16)
    scr("Ys", [SW // 128, 128, cfg.NK], BF16, ext_in=(dbg == 2))
    scr("X1", [16, 128, D], F32)
    scr("X2", [16, 128, D], F32)
    scr("WdB", [FH, D], BF16)
    scr("WoB", [D, D], BF16)
    T.b = {}
    return T


DBG_LEVEL = 9
PRECAST = False
DBG_EVAC = 0
S5C_N = 8


def tb(T, name, idx=0):
    key = (name, idx)
    if key not in T.b:
        T.b[key] = Buf()
    return T.b[key]


def stage_adaln(P, cfg, T, per):
    D, KC = cfg.D, cfg.KC
    with Phase(P) as ph:
        cT, bcT = ph.sb("cT", [128, 2, KC], F32)
        for v in range(2):
            P.dma("sync", cT[:, v], T.cvec[v].rearrange("(kc p) -> p kc", p=128), writes=[bcT], strict=False,
                  allow_slow_non_contiguous=True)
        sT, bsT = ph.sb("sT", [128, KC, 2], F32)
        for v in range(2):
            P.I("scalar", "activation", [bcT], [bsT], out=sT[:, :, v], in_=cT[:, v, :], func=AF.Silu,
                strict=False)
        KH = min(KC, 16)
        NKH = KC // KH
        wring = ph.ring("adaw", 4, [128, KH, 512], F32)
        psr = ph.ring("adaps", 2, [2, 512], F32, psum=True)
        bring = ph.ring("adab", 3, [2, 512], F32)
        oring = ph.ring("adao", 3, [2, 512], F32)
        wv = T.ada_w.rearrange("(kc p) n -> p kc n", p=128)
        for nb in range(6 * D // 512):
            ps, bps = psr.get()
            for kh in range(NKH):
                w, bw = wring.get()
                P.dma("sync", w[:], wv[:, kh * KH:(kh + 1) * KH, nb * 512:(nb + 1) * 512], writes=[bw])
                P.mm(ps[:], [(sT[:, kh * KH + k, :], w[:, k, :]) for k in range(KH)], [bsT, bw], [bps],
                     start=(kh == 0), stop=(kh == NKH - 1))
            bt, bbt = bring.get()
            P.dma("gpsimd", bt[:], row_bcast(T.ada_b[0:1, nb * 512:(nb + 1) * 512], 2), writes=[bbt])
            o, bo = oring.get()
            P.I("vector", "tensor_tensor", [bps, bbt], [bo], out=o[:], in0=ps[:], in1=bt[:], op=ALU.add)
            P.dma("gpsimd", T.modscr[:, nb * 512:(nb + 1) * 512], o[:], reads=[bo], writes=[tb(T, "modscr")],
                  strict=False)
        modT, bmod = per["modT"]
        for v in range(2):
            for m_ in range(6):
                P.dma("sync", modT[:, v, m_], T.modscr[v, m_ * D:(m_ + 1) * D].rearrange("(kc p) -> p kc", p=128),
                      reads=[tb(T, "modscr")], writes=[bmod], strict=False, allow_slow_non_contiguous=True)
        ng, bng = ph.sb("ng", [128, 2, KC], F32)
        P.dma("sync", ng[:, 0], T.norm1_g[0].rearrange("(kc p) -> p kc", p=128), writes=[bng], strict=False,
              allow_slow_non_contiguous=True)
        P.dma("sync", ng[:, 1], T.norm2_g[0].rearrange("(kc p) -> p kc", p=128), writes=[bng], strict=False,
              allow_slow_non_contiguous=True)
        gs, bgs = per["gs"]
        for i, (v, m, n) in enumerate([(0, 1, 0), (1, 1, 0), (0, 4, 1)]):
            P.I("vector", "scalar_tensor_tensor", [bmod, bng], [bgs], out=gs[:, i], in0=modT[:, v, m],
                scalar=1.0, in1=ng[:, n], op0=ALU.add, op1=ALU.mult, strict=False)


def norm_tile(P, ph, x, bx, D, st_ring, junk):
    s_t, bst = st_ring.get()
    P.I("vector", "memset", [], [bst], s_t[:], 0.0)
    P.I("scalar", "activation", [bx], [junk[1], bst], out=junk[0][:, :D], in_=x, func=AF.Square,
        accum_out=s_t[:, 0:1])
    P.I("vector", "tensor_scalar", [bst], [bst], out=s_t[:, 1:2], in0=s_t[:, 0:1], scalar1=1.0 / D,
        scalar2=EPS, op0=ALU.mult, op1=ALU.add)
    P.I("scalar", "activation", [bst], [bst], out=s_t[:, 2:3], in_=s_t[:, 1:2], func=AF.Sqrt)
    P.I("vector", "reciprocal", [bst], [bst], out=s_t[:, 3:4], in_=s_t[:, 2:3])
    return s_t[:, 3:4], bst


def transpose_mod(P, xn, bxn, ncol0, nk, ident, tps, hb, bhb, tok0, ntok, gs, sh, bmods, kc0):
    for gi, k8 in enumerate(range(0, nk, 8)):
        tp, btp = tps.get()
        n8 = min(8, nk - k8)

        def fn(e, tp=tp, k8=k8, n8=n8):
            ins = None
            for j in range(n8):
                c0 = ncol0 + (k8 + j) * 128
                ins = e.transpose(tp[:, j, :ntok], xn[:ntok, c0:c0 + 128], ident[0][:ntok, :ntok])
            return ins
        P.op("tensor", fn, [bxn, ident[1]], [btp])
        P.tcount = getattr(P, "tcount", 0) + 1
        for j in range(n8):
            kc = kc0 + k8 + j
            if P.tcount % 2 == 0:
                P.I("vector", "tensor_scalar", [btp] + bmods, [bhb], out=hb[:, kc, tok0:tok0 + ntok],
                    in0=tp[:, j, :ntok], scalar1=gs[:, kc:kc + 1], scalar2=sh[:, kc:kc + 1],
                    op0=ALU.mult, op1=ALU.add, strict=False)
            else:
                P.I("scalar", "activation", [btp] + bmods, [bhb], out=hb[:, kc, tok0:tok0 + ntok],
                    in_=tp[:, j, :ntok], func=AF.Identity, scale=gs[:, kc:kc + 1], bias=sh[:, kc:kc + 1],
                    strict=False)


def stage_norm1(P, cfg, T, per):
    D, KC = cfg.D, cfg.KC
    modT, bmod = per["modT"]
    gs, bgs = per["gs"]
    with Phase(P) as ph:
        xr = ph.ring("x", 2, [128, D], F32)
        xnr = ph.ring("xn", 2, [128, D], BF16)
        junk = ph.sb("junk", [128, D], BF16)
        st = ph.ring("st", 4, [128, 4], F32)
        tps = ph.ring("tps", 4, [128, 8, 128], BF16, psum=True)
        hblk = ph.ring("hblk", 2, [128, KC, 512], BF16)
        xv = T.xb.rearrange("(r s) d -> s r d", s=64)
        for blk in range(cfg.NB + 1):
            lat = blk < cfg.NB
            nt = 4 if lat else 2
            hb, bhb = hblk.get()
            for ti in range(nt):
                x, bx = xr.get()
                src = xv[blk * 4 + ti] if lat else T.xctx[ti * 128:(ti + 1) * 128, :]
                P.dma("sync", x[:], src, writes=[bx])
                rstd, bst = norm_tile(P, ph, x[:], bx, D, st, junk)
                if DBG_LEVEL < 1:
                    continue
                xn, bxn = xnr.get()
                P.I("vector", "tensor_scalar", [bx, bst], [bxn], out=xn[:], in0=x[:], scalar1=rstd,
                    scalar2=None, op0=ALU.mult)
                vi = 0 if lat else 1
                if DBG_LEVEL < 2:
                    continue
                transpose_mod(P, xn, bxn, 0, KC, per["ident"], tps, hb, bhb, ti * 128, 128,
                              gs[:, vi], modT[:, vi, 0], [bmod, bgs], 0)
            if lat:
                P.dma("gpsimd", T.hTs[blk], hb[:], reads=[bhb], writes=[tb(T, "hTs", blk)])
            else:
                P.dma("gpsimd", T.hTc, hb[:, :, 0:256], reads=[bhb], writes=[tb(T, "hTc")])


def evac(P, i, reads, writes, out, in_, strict=True):
    if i % 2 == 0:
        P.I("scalar", "copy", reads, writes, out=out, in_=in_, strict=strict)
    else:
        P.I("vector", "tensor_copy", reads, writes, out=out, in_=in_, strict=strict)


def stage_win(P, cfg, T):
    D, KC, FW = cfg.D, cfg.KC, cfg.FW
    with Phase(P) as ph:
        wr = ph.ring("wsl", 2, [128, KC, 512], BF16)
        hr = ph.ring("hT", 3, [128, KC, 512], BF16)
        pr = ph.ring("zps", 4, [128, 512], F32, psum=True)
        sr = ph.ring("zst", 6, [128, 512], BF16)
        wv = T.w_in.rearrange("(kc p) n -> p kc n", p=128)
        NSL = D // 512
        jobs = [(sl, blk) for sl in range(NSL) for blk in range(cfg.NB + (1 if sl * 512 >= FW else 0))]

        def load_w(sl):
            w, bw = wr.get()
            P.dma("gpsimd", w[:], wv[:, :, sl * 512:(sl + 1) * 512], writes=[bw])
            return w, bw

        def load_h(job):
            blk = job[1]
            h, bh = hr.get()
            if blk < cfg.NB:
                P.dma("sync", h[:], T.hTs[blk], reads=[tb(T, "hTs", blk)], writes=[bh])
            else:
                P.dma("sync", h[:, :, 0:256], T.hTc, reads=[tb(T, "hTc")], writes=[bh])
            return h, bh
        pre = [("WoB", T.w_out, k) for k in range(KC)] + [("WdB", T.ffn_w_down, k) for k in range(cfg.FC)]
        if not PRECAST:
            pre = []

        def precast(n):
            for _ in range(n):
                if pre:
                    nm, src, k = pre.pop(0)
                    P.dma("gpsimd", getattr(T, nm)[k * 128:(k + 1) * 128, :], src[k * 128:(k + 1) * 128, :],
                          writes=[tb(T, nm, k)])
        wts = {0: load_w(0)}
        pend = load_h(jobs[0])
        ne = 0
        for ji, (sl, blk) in enumerate(jobs):
            precast(1)
            if blk == 0 and sl + 1 < NSL:
                wts[sl + 1] = load_w(sl + 1)
            w, bw = wts[sl]
            h, bh = pend
            if ji + 1 < len(jobs):
                pend = load_h(jobs[ji + 1])
            is_s5 = sl * 512 >= FW
            lat = blk < cfg.NB
            ntok = 512 if lat else 256
            for mc in range(4):
                ps, bps = pr.get()
                P.mm(ps[:, :ntok], [(w[:, kc, mc * 128:(mc + 1) * 128], h[:, kc, :ntok]) for kc in range(KC)],
                     [bw, bh], [bps])
                s_, bs_ = sr.get()
                col = sl * 512 + mc * 128
                ne += 1
                if not is_s5:
                    evac(P, ne, [bps], [bs_], s_[:], ps[:])
                    P.dma("gpsimd", T.zF[col // 128][:, blk * 512:(blk + 1) * 512], s_[:], reads=[bs_],
                          writes=[tb(T, "zF", (col // 128, blk))])
                else:
                    ch0 = col - FW
                    if lat:
                        evac(P, ne, [bps], [bs_], s_[:], ps[:])
                        P.dma("gpsimd", T.Dscr[ch0:ch0 + 128, blk * 4:(blk + 1) * 4, 0:128],
                              s_[:].rearrange("p (s r) -> p s r", s=4), reads=[bs_],
                              writes=[tb(T, "Dscr", ch0 // 128)], strict=False)
                    else:
                        evac(P, ne, [bps], [bs_], s_[:, 0:256].rearrange("p (s rc) -> p rc s", rc=4),
                             ps[:, 0:256].rearrange("p (rc s) -> p rc s", rc=4))
                        P.dma("gpsimd", T.Dscr[ch0:ch0 + 128, :, 128:132],
                              s_[:, 0:256].rearrange("p (s rc) -> p s rc", rc=4), reads=[bs_],
                              writes=[tb(T, "Dscr", ch0 // 128)], strict=False,
                              allow_slow_non_contiguous=True)
        precast(len(pre))


def build(cfg, upto=99, dbg=0):
    nc = bass.Bass("TRN2", target_bir_lowering=False)
    T = declare(nc, cfg, dbg)
    with ExitStack() as es:
        P = Prog(nc, es)
        KC = cfg.KC

        def persist(name, shape, dt):
            return es.enter_context(nc.sbuf_tensor(name, list(shape), dt)), Buf()
        per = {
            "modT": persist("modT", [128, 2, 6, KC], F32),
            "gs": persist("gs", [128, 3, KC], F32),
            "ident": persist("ident", [128, 128], BF16),
        }
        per["bmodB"] = Buf()
        per["bgsB"] = Buf()
        P.dma("sync", per["ident"][0][:], T.identb, writes=[per["ident"][1]])
        for i, st in enumerate(STAGES):
            if i > upto:
                break
            if st.__name__ in PER_STAGES:
                st(P, cfg, T, per)
            else:
                st(P, cfg, T)
        P.emit()
    return nc


PER_STAGES = set()
STAGES = [stage_adaln, stage_norm1, stage_win]


def dft_tables(cfg, q):
    L = cfg.L
    s = np.arange(64)[:, None]
    rt = np.arange(128)[None, :]
    n = (64 * ((rt + 32 * q) % 128) + s).astype(np.int64)
    ks = np.arange(64)[:, None]
    kr = np.arange(32)[None, :]
    k = (64 * (kr + 32 * q) + ks).reshape(-1).astype(np.int64)
    ang = (2.0 * np.pi / L) * ((n.reshape(-1)[:, None] * k[None, :]) % L).astype(np.float64)
    norm = 1.0 / math.sqrt(L * cfg.HD)
    tab = np.stack([np.cos(ang) * norm, -np.sin(ang) * norm], axis=1)
    tab = tab.reshape(64, 128, 2, 4, 512).transpose(3, 0, 1, 2, 4)
    return bf16(tab)


def chan_dft(cfg):
    j = np.arange(cfg.HD)
    ang = 2.0 * np.pi * ((j[:, None] * j[None, :]) % cfg.HD) / cfg.HD
    return np.concatenate([np.cos(ang), np.sin(ang)], axis=1).astype(np.float32)


def s5_consts(cfg, q):
    return np.zeros((128, S5C_N), np.float32)


def perm_rows(w):
    n = w.shape[0] // 128
    return np.ascontiguousarray(w.reshape(n, 8, 16, -1).transpose(0, 2, 1, 3).reshape(w.shape))


def make_in_map(cfg, inp, core):
    b, q = core // 4, core % 4
    D = cfg.D
    f = lambda a: np.ascontiguousarray(np.asarray(a, dtype=np.float32))
    xg = f(inp["x"][b]).reshape(128, 64, D)
    xrot = np.roll(xg, -32 * q, axis=0)
    xown = np.ascontiguousarray(xrot[:32].transpose(1, 0, 2)).reshape(cfg.NK, D)
    G = cfg.G
    m = {
        "xb": np.ascontiguousarray(xrot.reshape(cfg.L, D)),
        "xctx": f(inp["ctx"][b]),
        "xown": xown,
        "cvec": np.stack([f(inp["c"][b]), f(inp["c_ctx"])]),
        "ada_w": f(inp["ada_w"][0]), "ada_b": f(inp["ada_b"][0]).reshape(1, -1),
        "norm1_g": f(inp["norm1_g"][0]).reshape(1, -1), "norm2_g": f(inp["norm2_g"][0]).reshape(1, -1),
        "final_g": f(inp["final_g"]).reshape(1, -1),
        "w_in": f(inp["w_in"][0]), "w_out": f(inp["w_out"][0]),
        "fourier_w": f(inp["fourier_w"][0]).reshape(cfg.FW, cfg.HD),
        "lam_re": f(inp["s5_lam_re"][0]).reshape(2 * G, 64), "lam_im": f(inp["s5_lam_im"][0]).reshape(2 * G, 64),
        "log_dt": f(inp["s5_log_dt"][0]).reshape(1, 2 * G),
        "b_re": f(inp["s5_b_re"][0]).reshape(2 * G * 64, 16), "b_im": f(inp["s5_b_im"][0]).reshape(2 * G * 64, 16),
        "c_re": f(inp["s5_c_re"][0]).reshape(2 * G * 16, 64), "c_im": f(inp["s5_c_im"][0]).reshape(2 * G * 16, 64),
        "s5_d": f(inp["s5_d"][0]).reshape(1, -1),
        "glu_w_a": perm_rows(f(inp["glu_w_a"][0])), "glu_b_a": f(inp["glu_b_a"][0]).reshape(1, -1),
        "glu_w_b": perm_rows(f(inp["glu_w_b"][0])), "glu_b_b": f(inp["glu_b_b"][0]).reshape(1, -1),
        "ffn_w_gate": f(inp["ffn_w_gate"][0]), "ffn_w_up": f(inp["ffn_w_up"][0]),
        "ffn_w_down": f(inp["ffn_w_down"][0]),
        "cdft": chan_dft(cfg), "tdft": dft_tables(cfg, q),
        "identb": bf16(np.eye(128)), "s5c": s5_consts(cfg, q),
    }
    return m


def stage_fourier_ab(P, cfg, T):
    HD, HC, NH, FW, FCH = cfg.HD, cfg.HC, cfg.NH, cfg.FW, cfg.FCH
    with Phase(P) as ph:
        cd, bcd = ph.sb("cd", [128, HC, 2 * HD], F32)
        P.dma("sync", cd[:], T.cdft.rearrange("(c p) n -> p c n", p=128), writes=[bcd])
        wfr = ph.ring("wf", 2, [128, HC, HD], F32)
        pq, bpq = ph.sb("pq", [128, NH, 2, HC, HD], BF16)
        aps = ph.ring("abps", 6, [128, 512], F32, psum=True)
        ne = 0
        for h in range(NH):
            wf, bwf = wfr.get()
            P.dma("sync", wf[:], T.fourier_w[h * HD:(h + 1) * HD].rearrange("(c p) e -> p c e", p=128),
                  writes=[bwf])
            for ab in range(2):
                for jc in range(HC):
                    ps, bps = aps.get()
                    P.mm(ps[:, :HD], [(cd[:, kc, ab * HD + jc * 128: ab * HD + (jc + 1) * 128], wf[:, kc, :])
                                      for kc in range(HC)], [bcd, bwf], [bps])
                    ne += 1
                    evac(P, ne, [bps], [bpq], pq[:, h, ab, jc, :], ps[:, :HD], strict=False)
        zr = ph.ring("zFl", 2, [128, FCH, 512], BF16)
        abst = ph.ring("abst", 2, [128, 2, FW], BF16)
        zv = T.zF.rearrange("c p t -> p c t")
        for blk in range(cfg.NB):
            z, bz = zr.get()
            P.dma("sync", z[:], zv[:, :, blk * 512:(blk + 1) * 512],
                  reads=[tb(T, "zF", (c, blk)) for c in range(FCH)], writes=[bz])
            for ti in range(4):
                s = blk * 4 + ti
                ab_t, bab = abst.get()
                for h in range(NH):
                    for ab in range(2):
                        ps, bps = aps.get()
                        P.mm(ps[:, :HD], [(z[:, h * HC + c, ti * 128:(ti + 1) * 128], pq[:, h, ab, c, :])
                                          for c in range(HC)], [bz, bpq], [bps])
                        ne += 1
                        evac(P, ne, [bps], [bab], ab_t[:, ab, h * HD:(h + 1) * HD], ps[:, :HD], strict=False)
                P.dma("gpsimd", T.ABs[s], ab_t[:], reads=[bab], writes=[tb(T, "ABs", s)])


def stage_dft(P, cfg, T):
    FCH = cfg.FCH
    CG = min(8, FCH)
    with Phase(P) as ph:
        abr = ph.ring("ab", 3, [128, 2, CG * 128], BF16)
        tr = ph.ring("tab", 3, [128, 2, 512], BF16)
        psb = [ph.ps("dps", [128, 512]) for _ in range(CG)]
        yst = ph.ring("yst", 3, [128, 512], BF16)
        ne = 0
        for cg in range(FCH // CG):
            for kb in range(4):
                for n_ in range(64):
                    a, ba = abr.get()
                    t, bt = tr.get()
                    P.dma("sync", a[:], T.ABs[n_][:, :, cg * CG * 128:(cg + 1) * CG * 128],
                          reads=[tb(T, "ABs", n_)], writes=[ba])
                    P.dma("sync", t[:], T.tdft[kb, n_], writes=[bt])

                    def fn(e, a=a, t=t, n_=n_):
                        ins = None
                        for ch in range(CG):
                            e.matmul(psb[ch][0][:], a[:, 0, ch * 128:(ch + 1) * 128], t[:, 0, :],
                                     start=(n_ == 0), stop=False)
                            ins = e.matmul(psb[ch][0][:], a[:, 1, ch * 128:(ch + 1) * 128], t[:, 1, :],
                                           start=False, stop=(n_ == 63))
                        return ins
                    P.op("tensor", fn, [ba, bt], [psb[ch][1] for ch in range(CG)])
                for ch in range(CG):
                    y, by = yst.get()
                    ne += 1
                    evac(P, ne, [psb[ch][1]], [by], y[:], psb[ch][0][:])
                    P.dma("gpsimd", T.ycat[cg * CG + ch][:, kb * 512:(kb + 1) * 512], y[:], reads=[by],
                          writes=[tb(T, "ycat", (cg * CG + ch, kb))])


def stage_glu(P, cfg, T):
    SW, SCH, FCH, NK = cfg.SW, cfg.SW // 128, cfg.FCH, cfg.NK
    with Phase(P) as ph:
        g, bg = ph.sb("gT", [128, SCH, NK], BF16)
        P.dma("sync", g[:], T.Ys.rearrange("c p k -> p c k"), reads=[tb(T, "Ys", c) for c in range(SCH)],
              writes=[bg])
        bias, bbias = ph.sb("glub", [128, 2, SCH], F32)
        P.dma("sync", bias[:, 0], T.glu_b_a[0].rearrange("(c p) -> p c", p=128), writes=[bbias], strict=False,
              allow_slow_non_contiguous=True)
        P.dma("sync", bias[:, 1], T.glu_b_b[0].rearrange("(c p) -> p c", p=128), writes=[bbias], strict=False,
              allow_slow_non_contiguous=True)
        wr = ph.ring("gluw", 2, [128, 2, SCH, 256], BF16)
        pa = ph.ring("glupa", 2, [128, 512], F32, psum=True)
        pb = ph.ring("glupb", 2, [128, 512], F32, psum=True)
        sg = ph.ring("glusg", 2, [128, 512], F32)
        ost = ph.ring("gluo", 3, [128, 512], BF16)
        wa = T.glu_w_a.rearrange("(c p) n -> p c n", p=128)
        wb = T.glu_w_b.rearrange("(c p) n -> p c n", p=128)
        for sl in range(SW // 256):
            w, bw = wr.get()
            P.dma("gpsimd", w[:, 0], wa[:, :, sl * 256:(sl + 1) * 256], writes=[bw], strict=False)
            P.dma("gpsimd", w[:, 1], wb[:, :, sl * 256:(sl + 1) * 256], writes=[bw], strict=False)
            for mc in range(2):
                ch = sl * 2 + mc
                for kb in range(NK // 512):
                    a_ps, ba = pa.get()
                    b_ps, bb = pb.get()
                    rhs = lambda kc: g[:, kc, kb * 512:(kb + 1) * 512]
                    P.mm(a_ps[:], [(w[:, 0, kc, mc * 128:(mc + 1) * 128], rhs(kc)) for kc in range(SCH)],
                         [bw, bg], [ba])
                    P.mm(b_ps[:], [(w[:, 1, kc, mc * 128:(mc + 1) * 128], rhs(kc)) for kc in range(SCH)],
                         [bw, bg], [bb])
                    s_, bs_ = sg.get()
                    P.I("scalar", "activation", [bb, bbias], [bs_], out=s_[:], in_=b_ps[:], func=AF.Sigmoid,
                        bias=bias[:, 1, ch:ch + 1], scale=1.0)
                    o, bo = ost.get()
                    P.I("vector", "scalar_tensor_tensor", [ba, bs_, bbias], [bo], out=o[:], in0=a_ps[:],
                        scalar=bias[:, 0, ch:ch + 1], in1=s_[:], op0=ALU.add, op1=ALU.mult)
                    P.dma("sync", T.ycat[FCH + ch][:, kb * 512:(kb + 1) * 512], o[:], reads=[bo],
                          writes=[tb(T, "ycat", (FCH + ch, kb))])


def gemm_tm(P, ph, cfg, act, bact, nK, W, wname, T, wring, psb, epilogue):
    D = cfg.D
    for ds in range(D // 1024):
        for kc in range(nK):
            w, bw = wring.get()
            P.dma("gpsimd", w[:], W[kc * 128:(kc + 1) * 128, ds * 1024:(ds + 1) * 1024],
                  reads=([tb(T, wname, kc)] if PRECAST else []), writes=[bw])

            def fn(e, w=w, kc=kc):
                ins = None
                for ti in range(4):
                    for hf in range(2):
                        ins = e.matmul(psb[ti * 2 + hf][0][:], act[:, kc, ti * 128:(ti + 1) * 128],
                                       w[:, hf * 512:(hf + 1) * 512], start=(kc == 0), stop=(kc == nK - 1))
                return ins
            P.op("tensor", fn, [bact, bw], [p[1] for p in psb])
        for ti in range(4):
            epilogue(ds, ti, [psb[ti * 2][0], psb[ti * 2 + 1][0]], [psb[ti * 2][1], psb[ti * 2 + 1][1]])


def stage_back(P, cfg, T, per):
    D, KC, FH, FC, NK = cfg.D, cfg.KC, cfg.FH, cfg.FC, cfg.NK
    modT = per["modT"][0]
    gs = per["gs"][0]
    bmod, bgs = per["bmodB"], per["bgsB"]
    with Phase(P) as ph:
        aT, baT = ph.sb("aT", [128, max(FC, KC), 512], BF16)
        h2T, bh2 = ph.sb("h2T", [128, KC, 512], BF16)
        wring = ph.ring("wtm", 6, [128, 1024], BF16)
        gur = ph.ring("guw", 2, [128, 2, KC, 128], BF16)
        psb = [ph.ps("bps", [128, 512]) for _ in range(8)]
        xr = ph.ring("xc", 2, [128, 1024], F32)
        gr = ph.ring("gc", 2, [128, 1024], F32)
        tr = ph.ring("tc", 3, [128, 1024], F32)
        xnr = ph.ring("xnc", 2, [128, 1024], BF16)
        junk = ph.sb("junkb", [128, 1024], BF16)
        sgr = ph.ring("sgr", 2, [128, 512], F32)
        tps = Ring([(psb[i][0][:].bitcast(BF16).rearrange("p (a b) -> p a b", a=8), psb[i][1]) for i in (6, 7)])
        ycv = T.ycat.rearrange("c p k -> p c k")
        wg = T.ffn_w_gate.rearrange("(c p) n -> p c n", p=128)
        wu = T.ffn_w_up.rearrange("(c p) n -> p c n", p=128)
        for kb in range(NK // 512):
            st, bst = ph.sb("bst", [128, 4, 8], F32)
            P.I("vector", "memset", [], [bst], st[:], 0.0)
            P.dma("sync", aT[:, 0:KC, :], ycv[:, :, kb * 512:(kb + 1) * 512],
                  reads=[tb(T, "ycat", (c, kb)) for c in range(KC)], writes=[baT])

            def epi1(ds, ti, pst, bpst, kb=kb, st=st, bst=bst):
                tile = kb * 4 + ti
                x, bx = xr.get()
                P.dma("sync", x[:], T.xown[tile * 128:(tile + 1) * 128, ds * 1024:(ds + 1) * 1024], writes=[bx])
                gt, bg_ = gr.get()
                P.dma("sync", gt[:], row_bcast(T.modscr[0:1, 2 * D + ds * 1024: 2 * D + (ds + 1) * 1024], 128),
                      reads=[tb(T, "modscr")], writes=[bg_])
                t_, bt_ = tr.get()
                for hf in range(2):
                    P.I("vector", "tensor_tensor", [bpst[hf], bg_], [bt_], out=t_[:, hf * 512:(hf + 1) * 512],
                        in0=pst[hf][:], in1=gt[:, hf * 512:(hf + 1) * 512], op=ALU.mult, strict=False)
                P.I("gpsimd", "tensor_tensor", [bx], [bt_], out=t_[:], in0=t_[:], in1=x[:], op=ALU.add)
                P.I("scalar", "activation", [bt_], [junk[1], bst], out=junk[0][:], in_=t_[:], func=AF.Square,
                    accum_out=st[:, ti, ds:ds + 1], strict=False)
                P.dma("sync", T.X1[tile][:, ds * 1024:(ds + 1) * 1024], t_[:], reads=[bt_],
                      writes=[tb(T, "X1", tile)], strict=False)
            gemm_tm(P, ph, cfg, aT, baT, KC, T.WoB if PRECAST else T.w_out, "WoB", T, wring, psb, epi1)
            for ti in range(4):
                tile = kb * 4 + ti
                P.I("vector", "tensor_reduce", [bst], [bst], out=st[:, ti, 4:5], in_=st[:, ti, 0:D // 1024],
                    axis=AX.X, op=ALU.add)
                P.I("vector", "tensor_scalar", [bst], [bst], out=st[:, ti, 5:6], in0=st[:, ti, 4:5],
                    scalar1=1.0 / D, scalar2=EPS, op0=ALU.mult, op1=ALU.add)
                P.I("scalar", "activation", [bst], [bst], out=st[:, ti, 6:7], in_=st[:, ti, 5:6], func=AF.Sqrt)
                P.I("vector", "reciprocal", [bst], [bst], out=st[:, ti, 7:8], in_=st[:, ti, 6:7])
                for ds in range(D // 1024):
                    x1, bx1 = xr.get()
                    P.dma("sync", x1[:], T.X1[tile][:, ds * 1024:(ds + 1) * 1024], reads=[tb(T, "X1", tile)],
                          writes=[bx1])
                    xn, bxn = xnr.get()
                    P.I("vector", "tensor_scalar", [bx1, bst], [bxn], out=xn[:], in0=x1[:],
                        scalar1=st[:, ti, 7:8], scalar2=None, op0=ALU.mult)
                    transpose_mod(P, xn, bxn, 0, 8, per["ident"], tps, h2T, bh2, ti * 128, 128,
                                  gs[:, 2], modT[:, 0, 3], [bmod, bgs], ds * 8)
            for f in range(FC):
                w, bw = gur.get()
                P.dma("gpsimd", w[:, 0], wg[:, :, f * 128:(f + 1) * 128], writes=[bw], strict=False)
                P.dma("gpsimd", w[:, 1], wu[:, :, f * 128:(f + 1) * 128], writes=[bw], strict=False)
                pg, bpg = psb[(f % 4) * 2]
                pu, bpu = psb[(f % 4) * 2 + 1]
                P.mm(pg[:], [(w[:, 0, kc, :], h2T[:, kc, :]) for kc in range(KC)], [bw, bh2], [bpg])
                P.mm(pu[:], [(w[:, 1, kc, :], h2T[:, kc, :]) for kc in range(KC)], [bw, bh2], [bpu])
                s_, bs_ = sgr.get()
                P.I("scalar", "activation", [bpg], [bs_], out=s_[:], in_=pg[:], func=AF.Silu)
                P.I("vector", "tensor_tensor", [bpu, bs_], [baT], out=aT[:, f, :], in0=pu[:], in1=s_[:],
                    op=ALU.mult, strict=False)

            def epi2(ds, ti, pst, bpst, kb=kb):
                tile = kb * 4 + ti
                x, bx = xr.get()
                P.dma("sync", x[:], T.X1[tile][:, ds * 1024:(ds + 1) * 1024], reads=[tb(T, "X1", tile)],
                      writes=[bx])
                gt, bg_ = gr.get()
                P.dma("sync", gt[:], row_bcast(T.modscr[0:1, 5 * D + ds * 1024: 5 * D + (ds + 1) * 1024], 128),
                      reads=[tb(T, "modscr")], writes=[bg_])
                t_, bt_ = tr.get()
                for hf in range(2):
                    P.I("vector", "tensor_tensor", [bpst[hf], bg_], [bt_], out=t_[:, hf * 512:(hf + 1) * 512],
                        in0=pst[hf][:], in1=gt[:, hf * 512:(hf + 1) * 512], op=ALU.mult, strict=False)
                P.I("gpsimd", "tensor_tensor", [bx], [bt_], out=t_[:], in0=t_[:], in1=x[:], op=ALU.add)
                P.dma("sync", T.X2[tile][:, ds * 1024:(ds + 1) * 1024], t_[:], reads=[bt_],
                      writes=[tb(T, "X2", tile)], strict=False)
            gemm_tm(P, ph, cfg, aT, baT, FC, T.WdB if PRECAST else T.ffn_w_down, "WdB", T, wring, psb, epi2)


def stage_final(P, cfg, T):
    D = cfg.D
    with Phase(P) as ph:
        fg, bfg = ph.sb("fg", [128, D], F32)
        P.dma("sync", fg[:], row_bcast(T.final_g[0:1, :], 128), writes=[bfg])
        xr = ph.ring("x2", 2, [128, D], F32)
        orr = ph.ring("o", 2, [128, D], F32)
        junk = ph.sb("junkf", [128, D], BF16)
        st = ph.ring("stf", 4, [128, 4], F32)
        for tile in range(16):
            x, bx = xr.get()
            P.dma("sync", x[:], T.X2[tile], reads=[tb(T, "X2", tile)], writes=[bx])
            rstd, bst = norm_tile(P, ph, x[:], bx, D, st, junk)
            o, bo = orr.get()
            P.I("vector", "scalar_tensor_tensor", [bx, bst, bfg], [bo], out=o[:], in0=x[:], scalar=rstd,
                in1=fg[:], op0=ALU.mult, op1=ALU.mult)
            P.dma("gpsimd", T.out[tile * 128:(tile + 1) * 128, :], o[:], reads=[bo])


STAGES = [stage_adaln, stage_norm1, stage_win, stage_fourier_ab, stage_dft, stage_glu, stage_back, stage_final]
PER_STAGES = {"stage_back"}


NM = 1 + 5 + 32 + 132
OFF_EVP = 0
OFF_EVQ = OFF_EVP + 128
OFF_EVM = OFF_EVQ + 128
OFF_MSK = OFF_EVM + 2 * NM
OFF_SGN = OFF_MSK + 2 * 132
OFF_MF = OFF_SGN + 4
OFF_MB = OFF_MF + 128
OFF_ID = OFF_MB + 128
S5C_N = OFF_ID + 128


def s5_consts(cfg, q):
    t = np.zeros((128, S5C_N), np.float64)
    s = np.arange(64)
    t[:, OFF_EVP:OFF_EVP + 64] = -s
    t[:, OFF_EVP + 64:OFF_EVP + 128] = -(63 - s)
    t[:, OFF_EVQ:OFF_EVQ + 64] = s
    t[:, OFF_EVQ + 64:OFF_EVQ + 128] = 63 - s
    R0 = 32 * q
    for d in range(2):
        ev = np.zeros(NM)
        msk = np.zeros(132)
        ev[0] = 1
        ev[1:6] = 64 * 2 ** np.arange(5)
        rho = np.arange(32)
        ev[6:38] = 64 * (rho if d == 0 else 31 - rho)
        for rp in range(132):
            if rp >= 128:
                rc = rp - 128
                idx = rc if d == 0 else 3 - rc
            else:
                r = (rp + R0) % 128
                idx = 4 + r if d == 0 else 4 + 127 - r
            enter = 4 + R0 if d == 0 else 4 + 127 - (R0 + 31)
            k = enter - idx - 1
            if k >= 0:
                ev[38 + rp] = 64 * k
                msk[rp] = 1.0
        t[:, OFF_EVM + d * NM:OFF_EVM + (d + 1) * NM] = ev
        t[:, OFF_MSK + d * 132:OFF_MSK + (d + 1) * 132] = msk
    hi = (np.arange(128) >= 64).astype(np.float64)
    t[:, OFF_SGN + 0] = 2 * hi - 1
    t[:, OFF_SGN + 1] = 1 - hi
    t[:, OFF_SGN + 2] = -hi
    t[:, OFF_SGN + 3] = -(1 - hi)
    sl = np.arange(128) // 16
    t[:, OFF_MF:OFF_MF + 128] = (sl[None, :] >= sl[:, None])
    t[:, OFF_MB:OFF_MB + 128] = (sl[:, None] >= sl[None, :])
    t[:, OFF_ID:OFF_ID + 128] = np.eye(128)
    return t.astype(np.float32)


def bc(ap, shape):
    return ap.broadcast_to(list(shape))


class S5:
    def __init__(self, P, cfg, T, ph):
        self.P, self.cfg, self.T = P, cfg, T
        G = cfg.G
        self.cst, self.bcst = ph.sb("s5c", [128, S5C_N], F32)
        P.dma("sync", self.cst[:], T.s5c, writes=[self.bcst])
        self.par, self.bpar = ph.sb("s5par", [128, 8, 2 * G], F32)
        par, bpar = self.par, self.bpar
        for h in range(2):
            for d in range(2):
                P.dma("sync", par[h * 64:(h + 1) * 64, 0, d * G:(d + 1) * G],
                      T.lam_re[d * G:(d + 1) * G].rearrange("g p -> p g"), writes=[bpar],
                      strict=False, allow_slow_non_contiguous=True)
                P.dma("sync", par[h * 64:(h + 1) * 64, 1, d * G:(d + 1) * G],
                      T.lam_im[d * G:(d + 1) * G].rearrange("g p -> p g"), writes=[bpar],
                      strict=False, allow_slow_non_contiguous=True)
        P.dma("sync", par[:, 2], row_bcast(T.log_dt[0:1, :], 128), writes=[bpar], strict=False)
        V = lambda *a, **k: P.I("vector", *a, **k)
        V("tensor_scalar_min", [bpar], [bpar], out=par[:, 0], in0=par[:, 0], scalar1=-1e-4)
        P.I("scalar", "activation", [bpar], [bpar], out=par[:, 2], in_=par[:, 2], func=AF.Exp)
        V("tensor_tensor", [bpar], [bpar], out=par[:, 3], in0=par[:, 0], in1=par[:, 2], op=ALU.mult)
        V("tensor_tensor", [bpar], [bpar], out=par[:, 4], in0=par[:, 1], in1=par[:, 2], op=ALU.mult)
        self.dcol, self.bdcol = ph.sb("dcol", [128, G], F32)
        for sl in range(8):
            P.dma("sync", self.dcol[sl * 16:(sl + 1) * 16, :], T.s5_d[0].rearrange("(g i) -> i g", i=16),
                  writes=[self.bdcol], strict=False, allow_slow_non_contiguous=True)

    def powers(self, ph, a, th, ev, n1, n, tag):
        P = self.P
        shp = [128, n1, n]
        ang, b0 = ph.sb("ang" + tag, shp, F32)
        w1, b1 = ph.sb("pw1" + tag, shp, F32)
        w2, b2 = ph.sb("pw2" + tag, shp, F32)
        ere, bre = ph.sb("ere" + tag, shp, F32)
        eim, bim = ph.sb("eim" + tag, shp, F32)
        rd = [self.bpar, self.bcst]
        a3 = bc(a.unsqueeze(2), shp)
        t3 = bc(th.unsqueeze(2), shp)
        P.I("vector", "tensor_tensor", rd, [b0], out=ang[:], in0=t3, in1=ev, op=ALU.mult)
        P.I("gpsimd", "tensor_tensor", rd, [bre], out=ere[:], in0=a3, in1=ev, op=ALU.mult)
        P.I("scalar", "activation", [bre], [bre], out=ere[:], in_=ere[:], func=AF.Exp)
        for (shift, dst, bdst) in ((0.0, w1, b1), (0.25, w2, b2)):
            P.I("vector", "tensor_scalar", [b0], [bdst], out=dst[:], in0=ang[:], scalar1=1.0 / TWO_PI,
                scalar2=shift, op0=ALU.mult, op1=ALU.add)
            P.I("vector", "tensor_scalar_add", [bdst], [bim], out=eim[:], in0=dst[:], scalar1=MAGIC)
            P.I("vector", "tensor_scalar_add", [bim], [bim], out=eim[:], in0=eim[:], scalar1=-MAGIC)
            P.I("vector", "tensor_tensor", [bdst, bim], [bdst], out=dst[:], in0=dst[:], in1=eim[:],
                op=ALU.subtract)
            P.I("scalar", "activation", [bdst], [bdst], out=dst[:], in_=dst[:], func=AF.Sin, scale=TWO_PI)
        P.I("vector", "tensor_tensor", [bre, b1], [bim], out=eim[:], in0=ere[:], in1=w1[:], op=ALU.mult)
        P.I("gpsimd", "tensor_tensor", [bre, b2], [bre], out=ere[:], in0=ere[:], in1=w2[:], op=ALU.mult)
        return (ere, bre), (eim, bim)

    def tmp(self, ph, key, shp):
        cache = ph.__dict__.setdefault("_tmpc", {})
        n = int(np.prod(shp[1:]))
        if key not in cache:
            cache[key] = ph.sb("tmp" + key, [128, max(n, getattr(ph, "_tmpn", n))], F32)
        t, b = cache[key]
        v = t[0:shp[0], 0:n]
        if len(shp) == 3:
            v = v.rearrange("p (a b) -> p a b", a=shp[1])
        elif len(shp) == 4:
            v = v.rearrange("p (a b c) -> p a b c", a=shp[1], b=shp[2])
        return v, b

    def cmul(self, ph, ore, bore, oim, boim, are, aim, ba, bre_, bim_, bb, shp, tag):
        P = self.P
        t1, bt1 = self.tmp(ph, "c1", shp)
        t2, bt2 = self.tmp(ph, "c2", shp)
        P.I("vector", "tensor_tensor", ba + bb, [bt1], out=t1, in0=aim, in1=bim_, op=ALU.mult)
        P.I("gpsimd", "tensor_tensor", ba + bb, [bt2], out=t2, in0=aim, in1=bre_, op=ALU.mult)
        P.I("vector", "tensor_tensor", ba + bb, [bore], out=ore, in0=are, in1=bre_, op=ALU.mult)
        P.I("gpsimd", "tensor_tensor", ba + bb, [boim], out=oim, in0=are, in1=bim_, op=ALU.mult)
        P.I("vector", "tensor_tensor", [bt1], [bore], out=ore, in0=ore, in1=t1, op=ALU.subtract)
        P.I("gpsimd", "tensor_tensor", [bt2], [boim], out=oim, in0=oim, in1=t2, op=ALU.add)

    def outer2(self, ph, out, bout, x1, x2, bx, c1, c2, bcb, n1, tag):
        P = self.P
        H = max(1, n1 // 4)
        shp = [128, H, 64, 16]
        ta, bta = ph.sb("o2a" + tag, shp, F32)
        tb_, btb = ph.sb("o2b" + tag, shp, F32)
        for k in range(0, n1, H):
            X1 = bc(x1[:, k:k + H].unsqueeze(3), shp)
            X2 = bc(x2[:, k:k + H].unsqueeze(3), shp)
            C1 = bc(c1[:, k:k + H].unsqueeze(2), shp)
            C2 = bc(c2[:, k:k + H].unsqueeze(2), shp)
            P.I("vector", "tensor_tensor", bx + bcb, [bta], out=ta[:], in0=X1, in1=C1, op=ALU.mult)
            P.I("gpsimd", "tensor_tensor", bx + bcb, [btb], out=tb_[:], in0=X2, in1=C2, op=ALU.mult)
            P.I("vector", "tensor_tensor", [bta, btb], [bout], out=out[:, k:k + H], in0=ta[:], in1=tb_[:],
                op=ALU.add, strict=False)


def stage_s5(P, cfg, T):
    G, GT, RP = cfg.G, cfg.GT, cfg.RP
    GELU_C = 2.0 * math.sqrt(2.0 / math.pi)
    with Phase(P) as ph0:
        S = S5(P, cfg, T, ph0)
        cst, bcst, par, bpar = S.cst, S.bcst, S.par, S.bpar
        identf = cst[:, OFF_ID:OFF_ID + 128]
        ident_b, bident_b = ph0.sb("identb2", [128, 128], BF16)
        P.dma("sync", ident_b[:], T.identb, writes=[bident_b])
        with Phase(P) as ph:
            ev1 = bc(cst[:, OFF_EVM:OFF_EVM + 1].unsqueeze(1), [128, 2 * G, 1])
            (e1r, be1r), (e1i, be1i) = S.powers(ph, par[:, 3], par[:, 4], ev1, 2 * G, 1, "k")
            nr, bnr = ph.sb("knr", [128, 2 * G], F32)
            den, bden = ph.sb("kden", [128, 2 * G], F32)
            t0, bt0 = ph.sb("kt0", [128, 2 * G], F32)
            V = lambda *a, **k: P.I("vector", *a, **k)
            V("tensor_scalar_add", [be1r], [bnr], out=nr[:], in0=e1r[:, :, 0], scalar1=-1.0)
            V("tensor_tensor", [bpar], [bden], out=den[:], in0=par[:, 0], in1=par[:, 0], op=ALU.mult)
            V("tensor_tensor", [bpar], [bt0], out=t0[:], in0=par[:, 1], in1=par[:, 1], op=ALU.mult)
            V("tensor_tensor", [bt0], [bden], out=den[:], in0=den[:], in1=t0[:], op=ALU.add)
            V("reciprocal", [bden], [bden], out=den[:], in_=den[:])
            V("tensor_tensor", [bnr, bpar], [bt0], out=t0[:], in0=nr[:], in1=par[:, 0], op=ALU.mult)
            V("tensor_tensor", [be1i, bpar], [bpar], out=par[:, 7], in0=e1i[:, :, 0], in1=par[:, 1], op=ALU.mult)
            V("tensor_tensor", [bt0, bpar], [bt0], out=t0[:], in0=t0[:], in1=par[:, 7], op=ALU.add)
            V("tensor_tensor", [bt0, bden], [bpar], out=par[:, 5], in0=t0[:], in1=den[:], op=ALU.mult)
            V("tensor_tensor", [be1i, bpar], [bt0], out=t0[:], in0=e1i[:, :, 0], in1=par[:, 0], op=ALU.mult)
            V("tensor_tensor", [bnr, bpar], [bpar], out=par[:, 7], in0=nr[:], in1=par[:, 1], op=ALU.mult)
            V("tensor_tensor", [bt0, bpar], [bt0], out=t0[:], in0=t0[:], in1=par[:, 7], op=ALU.subtract)
            V("tensor_tensor", [bt0, bden], [bpar], out=par[:, 6], in0=t0[:], in1=den[:], op=ALU.mult)
        pv = par[:].rearrange("p k (d g) -> p k d g", d=2)
        gen = dft_gen(P, cfg, T, ph0)

        def adv(k):
            for _ in range(k):
                if next(gen, "done") == "done":
                    break
        adv(1)
        for gt in range(GT):
            s5_subbatch(P, cfg, T, S, gt, pv, ident_b, bident_b, identf, GELU_C, adv)
        adv(1 << 30)


def s5_subbatch(P, cfg, T, S, gt, pv, ident_b, bident_b, identf, GELU_C, adv):
    G, RP = cfg.G, cfg.RP
    cst, bcst, par, bpar = S.cst, S.bcst, S.par, S.bpar
    g0 = gt * 8
    V = lambda *a, **k: P.I("vector", *a, **k)
    GP = lambda *a, **k: P.I("gpsimd", *a, **k)
    ACT = lambda *a, **k: P.I("scalar", *a, **k)

    def prm(k):
        return pv[:, k, :, g0:g0 + 8]

    def ev2(off, n, per_d):
        a = cst[:, off:off + 2 * per_d].rearrange("p (d n) -> p d n", d=2)[:, :, 0:n]
        return bc(a.unsqueeze(2), [128, 2, 8, n])

    with Phase(P) as ph:
        PQ, bPQ = ph.sb("PQ", [128, 16, 64, 16], BF16)
        Psb, bPsb = ph.sb("Psb", [128, 16, 8, 128], BF16)
        U, bU = ph.sb("U", [128, 8, 8, RP], BF16)
        Pt0, bPt0 = ph.sb("Pt0", [128, 16, 128], BF16)
        Z0p, bZ0p = ph.sb("Z0p", [128, 16, 32], F32)
        Dsb, bDsb = ph.sb("Dsb", [128, 8, 128], BF16)
        psr = Ring([ph.ps("s5ps", [128, 512], F32) for _ in range(4)])
        adv(20)
        dv = T.Dscr.rearrange("c (j sl) r -> c sl j r", sl=8)
        for gl in range(8):
            for sl in range(8):
                ch0 = (g0 + gl) * 16
                P.dma("sync", U[sl * 16:(sl + 1) * 16, gl], dv[ch0:ch0 + 16, sl],
                      reads=[tb(T, "Dscr", gt)], writes=[bU], strict=False)
        with Phase(P) as p1:
            p1._tmpn = 16 * 64
            Bx, bBx = p1.sb("Bx", [128, 2, 8, 16], F32)
            By, bBy = p1.sb("By", [128, 2, 8, 16], F32)
            brv = T.b_re.rearrange("(d g p) i -> p d g i", d=2, g=G)[:, :, g0:g0 + 8, :]
            biv = T.b_im.rearrange("(d g p) i -> p d g i", d=2, g=G)[:, :, g0:g0 + 8, :]
            for d in range(2):
                P.dma("sync", Bx[0:64, d], brv[:, d], writes=[bBx], strict=False)
                P.dma("sync", Bx[64:128, d], biv[:, d], writes=[bBx], strict=False)
                P.dma("sync", By[0:64, d], biv[:, d], writes=[bBy], strict=False)
                P.dma("sync", By[64:128, d], brv[:, d], writes=[bBy], strict=False)
            (er, ber), (ei, bei) = S.powers(p1, prm(3).rearrange("p d g -> p (d g)") if False else prm(3),
                                            prm(4), ev2(OFF_EVP, 64, 64), 16, 64, "p")                 if False else s5_pow(S, p1, prm, ev2(OFF_EVP, 64, 64), 64, "p")
            w1, bw1 = p1.sb("w1", [128, 2, 8, 64], F32)
            w2, bw2 = p1.sb("w2", [128, 2, 8, 64], F32)
            shp = [128, 2, 8, 64]
            S.cmul(p1, w1[:], bw1, w2[:], bw2, bc(prm(5).unsqueeze(3), shp), bc(prm(6).unsqueeze(3), shp),
                   [bpar], er[:], ei[:], [ber, bei], shp, "w")
            V("tensor_scalar", [bw2, bcst], [bw2], out=w2[:], in0=w2[:], scalar1=cst[:, OFF_SGN:OFF_SGN + 1],
              scalar2=None, op0=ALU.mult)
            S.outer2(p1, PQ[:], bPQ, w1[:].rearrange("p d g s -> p (d g) s"),
                     w2[:].rearrange("p d g s -> p (d g) s"), [bw1, bw2],
                     Bx[:].rearrange("p d g i -> p (d g) i"), By[:].rearrange("p d g i -> p (d g) i"),
                     [bBx, bBy], 16, "p")
        V("tensor_copy", [bPQ], [bPt0], out=Pt0[:], in_=PQ[:].rearrange("p a s i -> p a (s i)")[:, :, 0:128])
        PQf = PQ[:].rearrange("p a s i -> p a (s i)")
        for gd in range(16):
            ps, bps = psr.get()
            tpv = ps[:].bitcast(BF16).rearrange("p (a b) -> p a b", a=8)

            def fn(e, gd=gd, tpv=tpv):
                ins = None
                for j in range(8):
                    ins = e.transpose(tpv[:, j, :], PQf[:, gd, j * 128:(j + 1) * 128], ident_b[:])
                return ins
            P.op("tensor", fn, [bPQ, bident_b], [bps])
            evac(P, gd, [bps], [bPsb], Psb[:, gd], tpv, strict=False)
        with Phase(P) as p2:
            p2._tmpn = 16 * RP
            adv(30)
            tot, btot = p2.sb("tot", [64, 2, 16, RP], F32)
            for gd in range(16):
                ps, bps = psr.get()
                pvw = ps[0:64, 0:2 * RP].rearrange("p (c r) -> p c r", c=2)
                for c in range(2):
                    P.mm(pvw[:, c, :], [(Psb[:, gd, j, c * 64:(c + 1) * 64], U[:, gd % 8, j, :]) for j in range(8)],
                         [bPsb, bU], [bps], strict=False)
                evac(P, gd, [bps], [btot], tot[:, :, gd, :], pvw, strict=False)
            (mr, bmr), (mi, bmi) = s5_pow(S, p2, prm, ev2(OFF_EVM, NM, NM), NM, "m")
            mr64 = mr[0:64].rearrange("p d g n -> p (d g) n")
            mi64 = mi[0:64].rearrange("p d g n -> p (d g) n")
            (lr, blr), (li, bli) = s5_pow(S, p2, prm, bc(cst[:, OFF_EVQ + 64:OFF_EVQ + 65].unsqueeze(1).unsqueeze(1),
                                                        [128, 2, 8, 1]), 1, "l")
            Tr, bTr = p2.sb("Tr", [64, 16, RP], F32)
            Ti, bTi = p2.sb("Ti", [64, 16, RP], F32)
            shp = [64, 16, RP]
            l63r = bc(lr[0:64].rearrange("p d g n -> p (d g) n"), shp)
            l63i = bc(li[0:64].rearrange("p d g n -> p (d g) n"), shp)
            S.cmul(p2, Tr[:], bTr, Ti[:], bTi, l63r, l63i, [blr, bli], tot[:, 0], tot[:, 1], [btot], shp, "t")
            mskv = bc(cst[0:64, OFF_MSK:OFF_MSK + 2 * RP].rearrange("p (d n) -> p d n", d=2).unsqueeze(2),
                      [64, 2, 8, RP])
            V("tensor_tensor", [bcst], [bmr], out=mr[0:64, :, :, 38:38 + RP], in0=mr[0:64, :, :, 38:38 + RP],
              in1=mskv, op=ALU.mult)
            GP("tensor_tensor", [bcst], [bmi], out=mi[0:64, :, :, 38:38 + RP], in0=mi[0:64, :, :, 38:38 + RP],
               in1=mskv, op=ALU.mult)
            bHr, bHi = Buf(), Buf()
            S.cmul(p2, tot[:, 0], btot, tot[:, 1], btot, mr64[:, :, 38:38 + RP], mi64[:, :, 38:38 + RP],
                   [bmr, bmi], Tr[:], Ti[:], [bTr, bTi], shp, "h")
            hin, bhin = p2.sb("hin", [64, 2, 16], F32)
            V("tensor_reduce", [btot], [bhin], out=hin[:, 0], in_=tot[:, 0], axis=AX.X, op=ALU.add, strict=False)
            V("tensor_reduce", [btot], [bhin], out=hin[:, 1], in_=tot[:, 1], axis=AX.X, op=ALU.add, strict=False)
            yr, byr = p2.sb("yr", [64, 16, 32], F32)
            yi, byi = p2.sb("yi", [64, 16, 32], F32)
            V("tensor_copy", [bTr], [byr], out=yr[:], in_=Tr[:, :, 0:32])
            GP("tensor_copy", [bTi], [byi], out=yi[:], in_=Ti[:, :, 0:32])
            ur, bur = p2.sb("ur", [64, 16, 32], F32)
            ui, bui = p2.sb("ui", [64, 16, 32], F32)
            for m in range(5):
                sh = 1 << m
                n = 32 - sh
                for d in range(2):
                    gsl = slice(d * 8, (d + 1) * 8)
                    src = slice(0, n) if d == 0 else slice(sh, 32)
                    dst = slice(sh, 32) if d == 0 else slice(0, n)
                    shp2 = [64, 8, n]
                    ar = bc(mr64[:, gsl, 1 + m:2 + m], shp2)
                    ai = bc(mi64[:, gsl, 1 + m:2 + m], shp2)
                    S.cmul(p2, ur[:, gsl, 0:n], bur, ui[:, gsl, 0:n], bui, ar, ai, [bmr, bmi],
                           yr[:, gsl, src], yi[:, gsl, src], [byr, byi], shp2, f"s{m}{d}")
                    V("tensor_tensor", [bur], [byr], out=yr[:, gsl, dst], in0=yr[:, gsl, dst], in1=ur[:, gsl, 0:n],
                      op=ALU.add)
                    GP("tensor_tensor", [bui], [byi], out=yi[:, gsl, dst], in0=yi[:, gsl, dst], in1=ui[:, gsl, 0:n],
                       op=ALU.add)
            er_, ber_, ei_, bei_ = ur, bur, ui, bui
            shp3 = [64, 16, 32]
            S.cmul(p2, er_[:], ber_, ei_[:], bei_, mr64[:, :, 6:38], mi64[:, :, 6:38], [bmr, bmi],
                   bc(hin[:, 0].unsqueeze(2), shp3), bc(hin[:, 1].unsqueeze(2), shp3), [bhin], shp3, "e")
            for d in range(2):
                gsl = slice(d * 8, (d + 1) * 8)
                src = slice(0, 31) if d == 0 else slice(1, 32)
                dst = slice(1, 32) if d == 0 else slice(0, 31)
                V("tensor_tensor", [byr], [ber_], out=er_[:, gsl, dst], in0=er_[:, gsl, dst], in1=yr[:, gsl, src],
                  op=ALU.add)
                GP("tensor_tensor", [byi], [bei_], out=ei_[:, gsl, dst], in0=ei_[:, gsl, dst], in1=yi[:, gsl, src],
                   op=ALU.add)
            z0r, bz0r = p2.sb("z0r", [64, 16, 32], F32)
            z0i, bz0i = p2.sb("z0i", [64, 16, 32], F32)
            S.cmul(p2, z0r[:], bz0r, z0i[:], bz0i, bc(mr64[:, :, 0:1], shp3), bc(mi64[:, :, 0:1], shp3),
                   [bmr, bmi], er_[:], ei_[:], [ber_, bei_], shp3, "z")
            P.dma("sync", Z0p[0:64], z0r[:], reads=[bz0r], writes=[bZ0p], strict=False)
            P.dma("sync", Z0p[64:128], z0i[:], reads=[bz0i], writes=[bZ0p], strict=False)
        adv(15)
        with Phase(P) as p3:
            Cx, bCx = p3.sb("Cx", [128, 2, 8, 16], F32)
            Cy, bCy = p3.sb("Cy", [128, 2, 8, 16], F32)
            cl, bcl = p3.sb("cl", [128, 2, 2, 2, 64], F32)
            for d in range(2):
                r0 = (d * G + g0) * 16
                for ri, src in enumerate((T.c_re, T.c_im)):
                    for h in range(2):
                        P.dma("sync", cl[:, d, ri, h], src[r0:r0 + 128, :], writes=[bcl], strict=False)
            for d in range(2):
                for ri, (dst, bdst) in enumerate(((Cx, bCx), (Cy, bCy))):
                    ps, bps = psr.get()
                    P.op("tensor", lambda e, ps=ps, d=d, ri=ri: e.transpose(
                        ps[:, 0:128], cl[:, d, ri].rearrange("p h q -> p (h q)"), identf), [bcl, bcst], [bps])
                    evac(P, d * 2 + ri, [bps], [bdst], dst[:, d].rearrange("p g o -> p (g o)"), ps[:, 0:128],
                         strict=False)
            (qr, bqr), (qi, bqi) = s5_pow(S, p3, prm, ev2(OFF_EVQ, 64, 64), 64, "q")
            x1, bx1 = p3.sb("x1", [128, 2, 8, 64], F32)
            x2, bx2 = p3.sb("x2", [128, 2, 8, 64], F32)
            sc = lambda k: cst[:, OFF_SGN + k:OFF_SGN + k + 1]
            V("tensor_scalar", [bqr, bcst], [bx1], out=x1[:], in0=qr[:], scalar1=sc(1), scalar2=None, op0=ALU.mult)
            V("scalar_tensor_tensor", [bqi, bcst, bx1], [bx1], out=x1[:], in0=qi[:], scalar=sc(2), in1=x1[:],
              op0=ALU.mult, op1=ALU.add)
            GP("tensor_scalar", [bqi, bcst], [bx2], out=x2[:], in0=qi[:], scalar1=sc(3), scalar2=None, op0=ALU.mult)
            V("scalar_tensor_tensor", [bqr, bcst, bx2], [bx2], out=x2[:], in0=qr[:], scalar=sc(2), in1=x2[:],
              op0=ALU.mult, op1=ALU.add)
            S.outer2(p3, PQ[:], bPQ, x1[:].rearrange("p d g s -> p (d g) s"),
                     x2[:].rearrange("p d g s -> p (d g) s"), [bx1, bx2],
                     Cx[:].rearrange("p d g i -> p (d g) i"), Cy[:].rearrange("p d g i -> p (d g) i"),
                     [bCx, bCy], 16, "q")
        with Phase(P) as p4:
            dps = [psr.get() for _ in range(4)]
            for d in range(2):
                for hf in range(2):
                    ps, bps = dps[d * 2 + hf]
                    for k in range(4):
                        gl = hf * 4 + k
                        gd = d * 8 + gl
                        P.mm(ps[:, k * 128:(k + 1) * 128], [(Pt0[:, gd, :], PQf[:, gd, 0:128])], [bPt0, bPQ], [bps],
                             strict=False)
            d1, bd1 = p4.sb("d1", [128, 8, 128], F32)
            d2, bd2 = p4.sb("d2", [128, 8, 128], F32)
            mf = bc(cst[:, OFF_MF:OFF_MF + 128].unsqueeze(1), [128, 4, 128])
            mb = bc(cst[:, OFF_MB:OFF_MB + 128].unsqueeze(1), [128, 4, 128])
            for hf in range(2):
                V("tensor_tensor", [dps[hf][1], bcst], [bd1], out=d1[:, hf * 4:(hf + 1) * 4],
                  in0=dps[hf][0][:].rearrange("p (k c) -> p k c", k=4), in1=mf, op=ALU.mult, strict=False)
                V("tensor_tensor", [dps[2 + hf][1], bcst], [bd2], out=d2[:, hf * 4:(hf + 1) * 4],
                  in0=dps[2 + hf][0][:].rearrange("p (k c) -> p k c", k=4), in1=mb, op=ALU.mult, strict=False)
            GP("tensor_tensor", [bd2], [bd1], out=d1[:], in0=d1[:], in1=d2[:], op=ALU.add)
            for gl in range(8):
                V("scalar_tensor_tensor", [bd1, bcst, S.bdcol], [bDsb], out=Dsb[:, gl], in0=identf,
                  scalar=S.dcol[:, g0 + gl:g0 + gl + 1], in1=d1[:, gl], op0=ALU.mult, op1=ALU.add, strict=False)
        with Phase(P) as p5:
            So, bSo = p5.sb("So", [128, 16, 8, 32], F32)
            Z, bZ = p5.sb("Z", [128, 16, 8, 32], BF16)
            for g2 in range(8):
                ps, bps = psr.get()
                for k in range(2):
                    gd = g2 * 2 + k
                    for j in range(8):
                        P.mm(ps[:, (k * 8 + j) * 32:(k * 8 + j + 1) * 32], [(Psb[:, gd, j, :], U[:, gd % 8, j, 0:32])],
                             [bPsb, bU], [bps], strict=False)
                evac(P, g2, [bps], [bSo], So[:, g2 * 2:(g2 + 1) * 2].rearrange("p a j r -> p (a j r)"), ps[:],
                     strict=False)
            for d in range(2):
                gsl = slice(d * 8, (d + 1) * 8)
                order = list(range(8)) if d == 0 else list(range(7, -1, -1))
                eng = V if d == 0 else GP
                eng("tensor_tensor", [bSo, bZ0p], [bSo], out=So[:, gsl, order[0]], in0=So[:, gsl, order[0]],
                    in1=Z0p[:, gsl], op=ALU.add)
                for a in range(1, 7):
                    eng("tensor_tensor", [bSo], [bSo], out=So[:, gsl, order[a]], in0=So[:, gsl, order[a]],
                        in1=So[:, gsl, order[a - 1]], op=ALU.add)
                eng("tensor_copy", [bZ0p], [bZ], out=Z[:, gsl, order[0]], in_=Z0p[:, gsl], strict=False)
                if d == 0:
                    eng("tensor_copy", [bSo], [bZ], out=Z[:, gsl, 1:8], in_=So[:, gsl, 0:7], strict=False)
                else:
                    eng("tensor_copy", [bSo], [bZ], out=Z[:, gsl, 0:7], in_=So[:, gsl, 1:8], strict=False)
            Yst, bYst = p5.sb("Yst", [128, 8, 8, 32], BF16)
            gtmp = Ring([p5.sb("gt", [128, 512], F32) for _ in range(4)])
            for g2 in range(4):
                ps, bps = psr.get()
                for k in range(2):
                    gl = g2 * 2 + k
                    for jt in range(8):
                        o = ps[:, (k * 8 + jt) * 32:(k * 8 + jt + 1) * 32]
                        P.mm(o, [(PQf[:, gl, jt * 128:(jt + 1) * 128], Z[:, gl, jt, :]),
                                 (PQf[:, 8 + gl, jt * 128:(jt + 1) * 128], Z[:, 8 + gl, jt, :]),
                                 (Dsb[:, gl, :], U[:, gl, jt, 0:32])], [bPQ, bZ, bDsb, bU], [bps], strict=False)
                x2_, bx2_ = gtmp.get()
                x3_, bx3_ = gtmp.get()
                ACT("activation", [bps], [bx2_], out=x2_[:], in_=ps[:], func=AF.Square)
                V("tensor_scalar", [bx2_], [bx2_], out=x2_[:], in0=x2_[:], scalar1=0.044715, scalar2=1.0,
                  op0=ALU.mult, op1=ALU.add)
                V("tensor_tensor", [bx2_, bps], [bx3_], out=x3_[:], in0=x2_[:], in1=ps[:], op=ALU.mult)
                ACT("activation", [bx3_], [bx3_], out=x3_[:], in_=x3_[:], func=AF.Sigmoid, scale=GELU_C)
                V("tensor_tensor", [bx3_, bps], [bYst], out=Yst[:, g2 * 2:(g2 + 1) * 2].rearrange("p a j r -> p (a j r)"),
                  in0=x3_[:], in1=ps[:], op=ALU.mult, strict=False)
            for tl in range(8):
                dst = bass.AP(T.Ys.tensor, gt * 128 * cfg.NK + tl * 32, [[8 * cfg.NK, 16], [256, 64], [1, 32]])
                P.dma("sync", dst, Yst[tl * 16:(tl + 1) * 16].rearrange("p g j r -> p (g j) r"), reads=[bYst],
                      writes=[tb(T, "Ys", gt)], strict=False)


def s5_pow(S, ph, prm, ev, n, tag):
    P = S.P
    shp = [128, 2, 8, n]
    ere, bre = ph.sb("ere" + tag, shp, F32)
    eim, bim = ph.sb("eim" + tag, shp, F32)
    with Phase(P) as pi:
        ang, b0 = pi.sb("ang" + tag, shp, F32)
        w1, b1 = pi.sb("pw1" + tag, shp, F32)
        w2, b2 = pi.sb("pw2" + tag, shp, F32)
        rd = [S.bpar, S.bcst]
        a3 = bc(prm(3).unsqueeze(3), shp)
        t3 = bc(prm(4).unsqueeze(3), shp)
        P.I("vector", "tensor_tensor", rd, [b0], out=ang[:], in0=t3, in1=ev, op=ALU.mult)
        P.I("gpsimd", "tensor_tensor", rd, [bre], out=ere[:], in0=a3, in1=ev, op=ALU.mult)
        P.I("scalar", "activation", [bre], [bre], out=ere[:], in_=ere[:], func=AF.Exp)
        for (shift, dst, bdst) in ((0.0, w1, b1), (0.25, w2, b2)):
            P.I("vector", "tensor_scalar", [b0], [bdst], out=dst[:], in0=ang[:], scalar1=1.0 / TWO_PI,
                scalar2=shift, op0=ALU.mult, op1=ALU.add)
            P.I("vector", "tensor_scalar_add", [bdst], [bim], out=eim[:], in0=dst[:], scalar1=MAGIC)
            P.I("vector", "tensor_scalar_add", [bim], [bim], out=eim[:], in0=eim[:], scalar1=-MAGIC)
            P.I("vector", "tensor_tensor", [bim], [bdst], out=dst[:], in0=dst[:], in1=eim[:], op=ALU.subtract)
            P.I("scalar", "activation", [bdst], [bdst], out=dst[:], in_=dst[:], func=AF.Sin, scale=TWO_PI)
        P.I("vector", "tensor_tensor", [bre, b1], [bim], out=eim[:], in0=ere[:], in1=w1[:], op=ALU.mult)
        P.I("gpsimd", "tensor_tensor", [b2], [bre], out=ere[:], in0=ere[:], in1=w2[:], op=ALU.mult)
    return (ere, bre), (eim, bim)


STAGES = [stage_adaln, stage_norm1, stage_win, stage_fourier_ab, stage_dft, stage_s5, stage_glu, stage_back,
          stage_final]


_CACHE = {}


def kernel(**inputs):
    cfg = Cfg()
    if "nc" not in _CACHE:
        _CACHE["nc"] = build(cfg, upto=99, dbg=0)
    nc = _CACHE["nc"]
    shared = None
    in_maps = []
    for core in range(8):
        m = make_in_map(cfg, inputs, core)
        if shared is None:
            shared = m
        else:
            for k in ("ada_w", "ada_b", "norm1_g", "norm2_g", "final_g", "w_in", "w_out", "fourier_w", "lam_re",
                      "lam_im", "log_dt", "b_re", "b_im", "c_re", "c_im", "s5_d", "glu_w_a", "glu_b_a", "glu_w_b",
                      "glu_b_b", "ffn_w_gate", "ffn_w_up", "ffn_w_down", "cdft", "identb"):
                m[k] = shared[k]
        in_maps.append(m)
    res = run_bass_kernel_spmd(nc, in_maps, core_ids=list(range(8)))
    D = cfg.D
    out = np.empty((2, 128, 64, D), np.float32)
    for core in range(8):
        b, q = core // 4, core % 4
        o = np.asarray(res.results[core]["out_own"], dtype=np.float32).reshape(64, 32, D)
        out[b, 32 * q:32 * q + 32] = o.transpose(1, 0, 2)
    return out.reshape(2, cfg.L, D)


def stage_front(P, cfg, T, per):
    D, KC = cfg.D, cfg.KC
    modT, bmodA = per["modT"]
    bmodB = per["bmodB"]
    gs, bgsA = per["gs"]
    bgsB = per["bgsB"]
    with Phase(P) as ph:
        cT, bcT = ph.sb("cT", [128, 2, KC], F32)
        for v in range(2):
            P.dma("sync", cT[:, v], T.cvec[v].rearrange("(kc p) -> p kc", p=128), writes=[bcT], strict=False,
                  allow_slow_non_contiguous=True)
        sT, bsT = ph.sb("sT", [128, KC, 2], F32)
        for v in range(2):
            P.I("scalar", "activation", [bcT], [bsT], out=sT[:, :, v], in_=cT[:, v, :], func=AF.Silu,
                strict=False)
        ng, bng = ph.sb("ng", [128, 2, KC], F32)
        P.dma("sync", ng[:, 0], T.norm1_g[0].rearrange("(kc p) -> p kc", p=128), writes=[bng], strict=False,
              allow_slow_non_contiguous=True)
        P.dma("sync", ng[:, 1], T.norm2_g[0].rearrange("(kc p) -> p kc", p=128), writes=[bng], strict=False,
              allow_slow_non_contiguous=True)
        KH = min(KC, 8)
        NKH = KC // KH
        wring = ph.ring("adaw", 4, [128, KH, 512], F32)
        psr = ph.ring("adaps", 2, [2, 512], F32, psum=True)
        bring = ph.ring("adab", 3, [2, 512], F32)
        oring = ph.ring("adao", 3, [2, 512], F32)
        wv = T.ada_w.rearrange("(kc p) n -> p kc n", p=128)
        NBLK = 6 * D // 512
        NEARLY = 2 * D // 512

        def ada_block(nb):
            ps, bps = psr.get()
            for kh in range(NKH):
                w, bw = wring.get()
                P.dma("gpsimd", w[:], wv[:, kh * KH:(kh + 1) * KH, nb * 512:(nb + 1) * 512], writes=[bw])
                P.mm(ps[:], [(sT[:, kh * KH + k, :], w[:, k, :]) for k in range(KH)], [bsT, bw], [bps],
                     start=(kh == 0), stop=(kh == NKH - 1))
            bt, bbt = bring.get()
            P.dma("scalar", bt[:], row_bcast(T.ada_b[0:1, nb * 512:(nb + 1) * 512], 2), writes=[bbt])
            o, bo = oring.get()
            P.I("vector", "tensor_tensor", [bps, bbt], [bo], out=o[:], in0=ps[:], in1=bt[:], op=ALU.add)
            key = "modscrA" if nb < NEARLY else "modscr"
            P.dma("scalar", T.modscr[:, nb * 512:(nb + 1) * 512], o[:], reads=[bo], writes=[tb(T, key)],
                  strict=False)

        def load_mod(ms, key, bm, bg_, gl):
            for v in range(2):
                for m_ in ms:
                    P.dma("sync", modT[:, v, m_], T.modscr[v, m_ * D:(m_ + 1) * D].rearrange("(kc p) -> p kc", p=128),
                          reads=[tb(T, key)], writes=[bm], strict=False, allow_slow_non_contiguous=True)
            for i, (v, m, n) in gl:
                P.I("vector", "scalar_tensor_tensor", [bm, bng], [bg_], out=gs[:, i], in0=modT[:, v, m],
                    scalar=1.0, in1=ng[:, n], op0=ALU.add, op1=ALU.mult, strict=False)
        for nb in range(NEARLY):
            ada_block(nb)
        load_mod((0, 1), "modscrA", bmodA, bgsA, [(0, (0, 1, 0)), (1, (1, 1, 0))])
        xr = ph.ring("x", 2, [128, D], F32)
        xnr = ph.ring("xn", 2, [128, D], BF16)
        junk = ph.sb("junk", [128, D], BF16)
        st = ph.ring("st", 4, [128, 4], F32)
        tps = ph.ring("tps", 4, [128, 8, 128], BF16, psum=True)
        hblk = ph.ring("hblk", 2, [128, KC, 512], BF16)
        xv = T.xb.rearrange("(r s) d -> s r d", s=64)
        nb_next = NEARLY
        tcount = 0
        for blk in range(cfg.NB + 1):
            lat = blk < cfg.NB
            nt = 4 if lat else 2
            hb, bhb = hblk.get()
            for ti in range(nt):
                x, bx = xr.get()
                src = xv[blk * 4 + ti] if lat else T.xctx[ti * 128:(ti + 1) * 128, :]
                P.dma("sync", x[:], src, writes=[bx])
                rstd, bst = norm_tile(P, ph, x[:], bx, D, st, junk)
                xn, bxn = xnr.get()
                P.I("vector", "tensor_scalar", [bx, bst], [bxn], out=xn[:], in0=x[:], scalar1=rstd,
                    scalar2=None, op0=ALU.mult)
                vi = 0 if lat else 1
                transpose_mod(P, xn, bxn, 0, KC, per["ident"], tps, hb, bhb, ti * 128, 128,
                              gs[:, vi], modT[:, vi, 0], [bmodA, bgsA], 0)
                tcount += 1
                if tcount % 2 == 0 and nb_next < NBLK:
                    ada_block(nb_next)
                    nb_next += 1
            if lat:
                P.dma("scalar", T.hTs[blk], hb[:], reads=[bhb], writes=[tb(T, "hTs", blk)])
            else:
                P.dma("scalar", T.hTc, hb[:, :, 0:256], reads=[bhb], writes=[tb(T, "hTc")])
        while nb_next < NBLK:
            ada_block(nb_next)
            nb_next += 1
        load_mod((2, 3, 4, 5), "modscr", bmodB, bgsB, [(2, (0, 4, 1))])


STAGES = [stage_front, stage_win, stage_fourier_ab, stage_dft, stage_s5, stage_glu, stage_back, stage_final]
PER_STAGES = {"stage_back", "stage_front"}


def dft_gen(P, cfg, T, ph):
    FCH = cfg.FCH
    CG = min(4, FCH)
    abr = ph.ring("ab", 3, [128, 2, CG * 128], BF16)
    tr = ph.ring("tab", 3, [128, 2, 512], BF16)
    psb = [ph.ps("dps", [128, 512]) for _ in range(CG)]
    yst = ph.ring("yst", 2, [128, 512], BF16)
    ne = 0
    for cg in range(FCH // CG):
        for kb in range(4):
            for n_ in range(64):
                a, ba = abr.get()
                t, bt = tr.get()
                P.dma("sync", a[:], T.ABs[n_][:, :, cg * CG * 128:(cg + 1) * CG * 128],
                      reads=[tb(T, "ABs", n_)], writes=[ba])
                P.dma("sync", t[:], T.tdft[kb, n_], writes=[bt])

                def fn(e, a=a, t=t, n_=n_):
                    ins = None
                    for ch in range(CG):
                        e.matmul(psb[ch][0][:], a[:, 0, ch * 128:(ch + 1) * 128], t[:, 0, :],
                                 start=(n_ == 0), stop=False)
                        ins = e.matmul(psb[ch][0][:], a[:, 1, ch * 128:(ch + 1) * 128], t[:, 1, :],
                                       start=False, stop=(n_ == 63))
                    return ins
                P.op("tensor", fn, [ba, bt], [psb[ch][1] for ch in range(CG)])
                yield
            for ch in range(CG):
                y, by = yst.get()
                ne += 1
                evac(P, ne, [psb[ch][1]], [by], y[:], psb[ch][0][:])
                P.dma("scalar", T.ycat[cg * CG + ch][:, kb * 512:(kb + 1) * 512], y[:], reads=[by],
                      writes=[tb(T, "ycat", (cg * CG + ch, kb))])
            yield


STAGES = [stage_front, stage_win, stage_fourier_ab, stage_s5, stage_glu, stage_back, stage_final]
```

```python
import math
import numpy as np
import ml_dtypes
from contextlib import ExitStack
import concourse.bass as bass
import concourse.mybir as mybir
from concourse.bass_utils import run_bass_kernel_spmd

F32 = mybir.dt.float32
BF16 = mybir.dt.bfloat16
ALU = mybir.AluOpType
AF = mybir.ActivationFunctionType
AX = mybir.AxisListType
NPOOL = 24
EPS = 1e-6
MAGIC = 12582912.0
TWO_PI = 2.0 * math.pi


class Buf:
    __slots__ = ("W", "R", "war", "mode")

    def __init__(self):
        self.W = []
        self.R = []
        self.war = []
        self.mode = "w"


def _prune(lst):
    if len(lst) > 48:
        best = {}
        for k, v in lst:
            if best.get(k, 0) < v:
                best[k] = v
        lst[:] = list(best.items())


class Ring:
    def __init__(self, items):
        self.items = items
        self.i = 0

    def get(self):
        it = self.items[self.i]
        self.i = (self.i + 1) % len(self.items)
        return it


class Prog:
    ENGS = ("tensor", "vector", "scalar", "gpsimd", "sync")

    def __init__(self, nc, es):
        self.nc = nc
        self.es = es
        self.sems = {}
        self.engs = {}
        for name in self.ENGS:
            self.sems["s_" + name] = es.enter_context(nc.semaphore("s_" + name))
            self.engs[name] = dict(key="s_" + name, cnt=0, ops=[], seen={})
        self.pool = {}
        self.pnext = {}
        for q in ("sync", "gpsimd", "scalar"):
            self.pool[q] = []
            for i in range(NPOOL):
                key = f"d_{q}_{i}"
                self.sems[key] = es.enter_context(nc.semaphore(key))
                self.pool[q].append(dict(key=key, val=0))
            self.pnext[q] = 0
        self.uid = 0

    def name(self, base):
        self.uid += 1
        return f"{base}_{self.uid}"

    def _deps(self, E, ename, reads, writes, extra=(), strict=True):
        need = {}

        def add(tok):
            k, v = tok
            if ename == "tensor" and k == "s_tensor":
                return
            if need.get(k, 0) < v:
                need[k] = v

        for b in reads:
            for t in b.W:
                add(t)
        for b in writes:
            if b.mode == "r":
                for t in b.R:
                    add(t)
                for t in b.W:
                    add(t)
            else:
                for t in b.war:
                    add(t)
                if strict:
                    for t in b.W:
                        add(t)
        for t in extra:
            add(t)
        waits = []
        for k, v in need.items():
            if E["seen"].get(k, 0) >= v:
                continue
            E["seen"][k] = v
            waits.append((k, v))
        return waits

    @staticmethod
    def _mark(tok, reads, writes, strict=True):
        for b in reads:
            if b.mode == "w":
                b.mode = "r"
                b.R = [tok]
            else:
                b.R.append(tok)
                _prune(b.R)
        for b in writes:
            if b.mode == "r":
                b.war = [tok] if strict else b.R
                b.R = []
                b.W = [tok]
                b.mode = "w"
            elif strict:
                b.W = [tok]
                b.war = [tok]
            else:
                b.W.append(tok)
                _prune(b.W)

    def op(self, eng, fn, reads=(), writes=(), strict=True):
        E = self.engs[eng]
        waits = self._deps(E, eng, reads, writes, strict=strict)
        E["cnt"] += 1
        tok = (E["key"], E["cnt"])
        self._mark(tok, reads, writes, strict)
        E["ops"].append((waits, fn, (E["key"], 1)))
        return tok

    def I(self, eng, method, reads, writes, *args, strict=True, **kw):
        return self.op(eng, lambda e: getattr(e, method)(*args, **kw), reads, writes, strict)

    def mm(self, out, pairs, reads, writes, start=True, stop=True, strict=True):
        pairs = list(pairs)

        def fn(e):
            n = len(pairs)
            ins = None
            for i, (l, r) in enumerate(pairs):
                ins = e.matmul(out, l, r, start=(start and i == 0), stop=(stop and i == n - 1))
            return ins
        return self.op("tensor", fn, reads, writes, strict)

    def dma(self, q, out, in_, reads=(), writes=(), strict=True, **kw):
        E = self.engs[q]
        slot = self.pool[q][self.pnext[q]]
        self.pnext[q] = (self.pnext[q] + 1) % NPOOL
        extra = [(slot["key"], slot["val"])] if slot["val"] else []
        waits = self._deps(E, q, reads, writes, extra, strict)
        slot["val"] += 16
        tok = (slot["key"], slot["val"])
        self._mark(tok, reads, writes, strict)
        E["ops"].append((waits, lambda e: e.dma_start(out=out, in_=in_, **kw), (slot["key"], 16)))
        return tok

    def all_tokens(self):
        toks = []
        for q in self.pool:
            for s in self.pool[q]:
                if s["val"]:
                    toks.append((s["key"], s["val"]))
        for name in ("tensor", "vector", "scalar", "gpsimd"):
            E = self.engs[name]
            if E["cnt"]:
                toks.append((E["key"], E["cnt"]))
        return toks

    def barrier(self):
        toks = self.all_tokens()
        for name in self.ENGS:
            E = self.engs[name]
            waits = []
            for k, v in toks:
                if name == "tensor" and k == "s_tensor":
                    continue
                if E["seen"].get(k, 0) >= v:
                    continue
                E["seen"][k] = v
                waits.append((k, v))
            if waits:
                E["ops"].append((waits, None, None))

    def emit(self):
        nc = self.nc
        sems = self.sems
        final = self.all_tokens()

        def mk(name, is_last):
            ops = self.engs[name]["ops"]

            def body(e):
                for waits, fn, inc in ops:
                    for k, v in waits:
                        e.wait_ge(sems[k], v)
                    if fn is not None:
                        fn(e).then_inc(sems[inc[0]], inc[1])
                if is_last:
                    for k, v in final:
                        e.wait_ge(sems[k], v)
            return body

        with nc.Block() as block:
            block.tensor(mk("tensor", False))
            block.vector(mk("vector", False))
            block.scalar(mk("scalar", False))
            block.gpsimd(mk("gpsimd", False))
            block.sync(mk("sync", True))


class Phase:
    def __init__(self, P):
        self.P = P
        self.es = ExitStack()

    def __enter__(self):
        self.es.__enter__()
        return self

    def __exit__(self, *a):
        self.P.barrier()
        return self.es.__exit__(*a)

    def sb(self, base, shape, dtype):
        t = self.es.enter_context(self.P.nc.sbuf_tensor(self.P.name(base), list(shape), dtype))
        return t, Buf()

    def ps(self, base, shape, dtype=F32):
        t = self.es.enter_context(self.P.nc.psum_tensor(self.P.name(base), list(shape), dtype))
        return t, Buf()

    def ring(self, base, n, shape, dtype, psum=False):
        return Ring([(self.ps if psum else self.sb)(base, shape, dtype) for _ in range(n)])


class Cfg:
    def __init__(self, D=4096, FH=11008):
        self.D = D
        self.KC = D // 128
        self.FW = D // 2
        self.NH = 4
        self.HD = self.FW // 4
        self.HC = self.HD // 128
        self.FCH = self.FW // 128
        self.SW = D // 2
        self.G = self.SW // 16
        self.GT = self.G // 8
        self.FH = FH
        self.FC = FH // 128
        self.NMOD = 6
        self.L = 8192
        self.CTX = 256
        self.NK = 2048
        self.NB = 16
        self.RP = 132


def bf16(a):
    return np.asarray(a, dtype=np.float32).astype(ml_dtypes.bfloat16)


class T_:
    pass


def row_bcast(ap, n):
    dims = [list(d) for d in ap.ap]
    return bass.AP(ap.tensor, ap.offset, [[0, n]] + dims[1:])


def declare(nc, cfg, dbg):
    T = T_()
    D, KC, FW, SW, FH, G = cfg.D, cfg.KC, cfg.FW, cfg.SW, cfg.FH, cfg.G

    def inp(name, shape, dt=F32):
        setattr(T, name, nc.dram_tensor(name, list(shape), dt, kind="ExternalInput").ap())

    def scr(name, shape, dt, ext_in=False):
        kind = "Internal"
        if dbg:
            kind = "ExternalInput" if ext_in else "ExternalOutput"
        setattr(T, name, nc.dram_tensor(name, list(shape), dt, kind=kind).ap())

    inp("xb", [cfg.L, D])
    inp("xctx", [cfg.CTX, D])
    inp("xown", [cfg.NK, D])
    inp("cvec", [2, D])
    inp("ada_w", [D, 6 * D])
    inp("ada_b", [1, 6 * D])
    inp("norm1_g", [1, D])
    inp("norm2_g", [1, D])
    inp("final_g", [1, D])
    inp("w_in", [D, D])
    inp("w_out", [D, D])
    inp("fourier_w", [FW, cfg.HD])
    inp("lam_re", [2 * G, 64])
    inp("lam_im", [2 * G, 64])
    inp("log_dt", [1, 2 * G])
    inp("b_re", [2 * G * 64, 16])
    inp("b_im", [2 * G * 64, 16])
    inp("c_re", [2 * G * 16, 64])
    inp("c_im", [2 * G * 16, 64])
    inp("s5_d", [1, SW])
    inp("glu_w_a", [SW, SW])
    inp("glu_b_a", [1, SW])
    inp("glu_w_b", [SW, SW])
    inp("glu_b_b", [1, SW])
    inp("ffn_w_gate", [D, FH])
    inp("ffn_w_up", [D, FH])
    inp("ffn_w_down", [FH, D])
    inp("cdft", [cfg.HD, 2 * cfg.HD])
    inp("tdft", [4, 64, 128, 2, 512], BF16)
    inp("identb", [128, 128], BF16)
    inp("s5c", [128, S5C_N])
    T.out = nc.dram_tensor("out_own", [cfg.NK, D], F32, kind="ExternalOutput").ap()
    scr("modscr", [2, 6 * D], F32)
    scr("hTs", [cfg.NB, 128, KC, 512], BF16)
    scr("hTc", [128, KC, 256], BF16)
    scr("zF", [cfg.FCH, 128, cfg.L], BF16)
    scr("Dscr", [SW, 64, cfg.RP], BF16)
    scr("ABs", [64, 128, 2, FW], BF16)
    scr("ycat", [KC, 128, cfg.NK], BF16)
    scr("Ys", [SW // 128, 128, cfg.NK], BF16, ext_in=(dbg == 2))
    scr("X1", [16, 128, D], F32)
    scr("X2", [16, 128, D], F32)
    scr("WdB", [FH, D], BF16)
    scr("WoB", [D, D], BF16)
    T.b = {}
    return T


DBG_LEVEL = 9
PRECAST = False
DBG_EVAC = 0
S5C_N = 8


def tb(T, name, idx=0):
    key = (name, idx)
    if key not in T.b:
        T.b[key] = Buf()
    return T.b[key]


def stage_adaln(P, cfg, T, per):
    D, KC = cfg.D, cfg.KC
    with Phase(P) as ph:
        cT, bcT = ph.sb("cT", [128, 2, KC], F32)
        for v in range(2):
            P.dma("sync", cT[:, v], T.cvec[v].rearrange("(kc p) -> p kc", p=128), writes=[bcT], strict=False,
                  allow_slow_non_contiguous=True)
        sT, bsT = ph.sb("sT", [128, KC, 2], F32)
        for v in range(2):
            P.I("scalar", "activation", [bcT], [bsT], out=sT[:, :, v], in_=cT[:, v, :], func=AF.Silu,
                strict=False)
        KH = min(KC, 16)
        NKH = KC // KH
        wring = ph.ring("adaw", 4, [128, KH, 512], F32)
        psr = ph.ring("adaps", 2, [2, 512], F32, psum=True)
        bring = ph.ring("adab", 3, [2, 512], F32)
        oring = ph.ring("adao", 3, [2, 512], F32)
        wv = T.ada_w.rearrange("(kc p) n -> p kc n", p=128)
        for nb in range(6 * D // 512):
            ps, bps = psr.get()
            for kh in range(NKH):
                w, bw = wring.get()
                P.dma("sync", w[:], wv[:, kh * KH:(kh + 1) * KH, nb * 512:(nb + 1) * 512], writes=[bw])
                P.mm(ps[:], [(sT[:, kh * KH + k, :], w[:, k, :]) for k in range(KH)], [bsT, bw], [bps],
                     start=(kh == 0), stop=(kh == NKH - 1))
            bt, bbt = bring.get()
            P.dma("gpsimd", bt[:], row_bcast(T.ada_b[0:1, nb * 512:(nb + 1) * 512], 2), writes=[bbt])
            o, bo = oring.get()
            P.I("vector", "tensor_tensor", [bps, bbt], [bo], out=o[:], in0=ps[:], in1=bt[:], op=ALU.add)
            P.dma("gpsimd", T.modscr[:, nb * 512:(nb + 1) * 512], o[:], reads=[bo], writes=[tb(T, "modscr")],
                  strict=False)
        modT, bmod = per["modT"]
        for v in range(2):
            for m_ in range(6):
                P.dma("sync", modT[:, v, m_], T.modscr[v, m_ * D:(m_ + 1) * D].rearrange("(kc p) -> p kc", p=128),
                      reads=[tb(T, "modscr")], writes=[bmod], strict=False, allow_slow_non_contiguous=True)
        ng, bng = ph.sb("ng", [128, 2, KC], F32)
        P.dma("sync", ng[:, 0], T.norm1_g[0].rearrange("(kc p) -> p kc", p=128), writes=[bng], strict=False,
              allow_slow_non_contiguous=True)
        P.dma("sync", ng[:, 1], T.norm2_g[0].rearrange("(kc p) -> p kc", p=128), writes=[bng], strict=False,
              allow_slow_non_contiguous=True)
        gs, bgs = per["gs"]
        for i, (v, m, n) in enumerate([(0, 1, 0), (1, 1, 0), (0, 4, 1)]):
            P.I("vector", "scalar_tensor_tensor", [bmod, bng], [bgs], out=gs[:, i], in0=modT[:, v, m],
                scalar=1.0, in1=ng[:, n], op0=ALU.add, op1=ALU.mult, strict=False)


def norm_tile(P, ph, x, bx, D, st_ring, junk):
    s_t, bst = st_ring.get()
    P.I("vector", "memset", [], [bst], s_t[:], 0.0)
    P.I("scalar", "activation", [bx], [junk[1], bst], out=junk[0][:, :D], in_=x, func=AF.Square,
        accum_out=s_t[:, 0:1])
    P.I("vector", "tensor_scalar", [bst], [bst], out=s_t[:, 1:2], in0=s_t[:, 0:1], scalar1=1.0 / D,
        scalar2=EPS, op0=ALU.mult, op1=ALU.add)
    P.I("scalar", "activation", [bst], [bst], out=s_t[:, 2:3], in_=s_t[:, 1:2], func=AF.Sqrt)
    P.I("vector", "reciprocal", [bst], [bst], out=s_t[:, 3:4], in_=s_t[:, 2:3])
    return s_t[:, 3:4], bst


def transpose_mod(P, xn, bxn, ncol0, nk, ident, tps, hb, bhb, tok0, ntok, gs, sh, bmods, kc0):
    for gi, k8 in enumerate(range(0, nk, 8)):
        tp, btp = tps.get()
        n8 = min(8, nk - k8)

        def fn(e, tp=tp, k8=k8, n8=n8):
            ins = None
            for j in range(n8):
                c0 = ncol0 + (k8 + j) * 128
                ins = e.transpose(tp[:, j, :ntok], xn[:ntok, c0:c0 + 128], ident[0][:ntok, :ntok])
            return ins
        P.op("tensor", fn, [bxn, ident[1]], [btp])
        P.tcount = getattr(P, "tcount", 0) + 1
        for j in range(n8):
            kc = kc0 + k8 + j
            if P.tcount % 2 == 0:
                P.I("vector", "tensor_scalar", [btp] + bmods, [bhb], out=hb[:, kc, tok0:tok0 + ntok],
                    in0=tp[:, j, :ntok], scalar1=gs[:, kc:kc + 1], scalar2=sh[:, kc:kc + 1],
                    op0=ALU.mult, op1=ALU.add, strict=False)
            else:
                P.I("scalar", "activation", [btp] + bmods, [bhb], out=hb[:, kc, tok0:tok0 + ntok],
                    in_=tp[:, j, :ntok], func=AF.Identity, scale=gs[:, kc:kc + 1], bias=sh[:, kc:kc + 1],
                    strict=False)


def stage_norm1(P, cfg, T, per):
    D, KC = cfg.D, cfg.KC
    modT, bmod = per["modT"]
    gs, bgs = per["gs"]
    with Phase(P) as ph:
        xr = ph.ring("x", 2, [128, D], F32)
        xnr = ph.ring("xn", 2, [128, D], BF16)
        junk = ph.sb("junk", [128, D], BF16)
        st = ph.ring("st", 4, [128, 4], F32)
        tps = ph.ring("tps", 4, [128, 8, 128], BF16, psum=True)
        hblk = ph.ring("hblk", 2, [128, KC, 512], BF16)
        xv = T.xb.rearrange("(r s) d -> s r d", s=64)
        for blk in range(cfg.NB + 1):
            lat = blk < cfg.NB
            nt = 4 if lat else 2
            hb, bhb = hblk.get()
            for ti in range(nt):
                x, bx = xr.get()
                src = xv[blk * 4 + ti] if lat else T.xctx[ti * 128:(ti + 1) * 128, :]
                P.dma("sync", x[:], src, writes=[bx])
                rstd, bst = norm_tile(P, ph, x[:], bx, D, st, junk)
                if DBG_LEVEL < 1:
                    continue
                xn, bxn = xnr.get()
                P.I("vector", "tensor_scalar", [bx, bst], [bxn], out=xn[:], in0=x[:], scalar1=rstd,
                    scalar2=None, op0=ALU.mult)
                vi = 0 if lat else 1
                if DBG_LEVEL < 2:
                    continue
                transpose_mod(P, xn, bxn, 0, KC, per["ident"], tps, hb, bhb, ti * 128, 128,
                              gs[:, vi], modT[:, vi, 0], [bmod, bgs], 0)
            if lat:
                P.dma("gpsimd", T.hTs[blk], hb[:], reads=[bhb], writes=[tb(T, "hTs", blk)])
            else:
                P.dma("gpsimd", T.hTc, hb[:, :, 0:256], reads=[bhb], writes=[tb(T, "hTc")])


def evac(P, i, reads, writes, out, in_, strict=True):
    if i % 2 == 0:
        P.I("scalar", "copy", reads, writes, out=out, in_=in_, strict=strict)
    else:
        P.I("vector", "tensor_copy", reads, writes, out=out, in_=in_, strict=strict)


def stage_win(P, cfg, T):
    D, KC, FW = cfg.D, cfg.KC, cfg.FW
    with Phase(P) as ph:
        wr = ph.ring("wsl", 2, [128, KC, 512], BF16)
        hr = ph.ring("hT", 3, [128, KC, 512], BF16)
        pr = ph.ring("zps", 4, [128, 512], F32, psum=True)
        sr = ph.ring("zst", 6, [128, 512], BF16)
        wv = T.w_in.rearrange("(kc p) n -> p kc n", p=128)
        NSL = D // 512
        jobs = [(sl, blk) for sl in range(NSL) for blk in range(cfg.NB + (1 if sl * 512 >= FW else 0))]

        def load_w(sl):
            w, bw = wr.get()
            P.dma("gpsimd", w[:], wv[:, :, sl * 512:(sl + 1) * 512], writes=[bw])
            return w, bw

        def load_h(job):
            blk = job[1]
            h, bh = hr.get()
            if blk < cfg.NB:
                P.dma("sync", h[:], T.hTs[blk], reads=[tb(T, "hTs", blk)], writes=[bh])
            else:
                P.dma("sync", h[:, :, 0:256], T.hTc, reads=[tb(T, "hTc")], writes=[bh])
            return h, bh
        pre = [("WoB", T.w_out, k) for k in range(KC)] + [("WdB", T.ffn_w_down, k) for k in range(cfg.FC)]
        if not PRECAST:
            pre = []

        def precast(n):
            for _ in range(n):
                if pre:
                    nm, src, k = pre.pop(0)
                    P.dma("gpsimd", getattr(T, nm)[k * 128:(k + 1) * 128, :], src[k * 128:(k + 1) * 128, :],
                          writes=[tb(T, nm, k)])
        wts = {0: load_w(0)}
        pend = load_h(jobs[0])
        ne = 0
        for ji, (sl, blk) in enumerate(jobs):
            precast(1)
            if blk == 0 and sl + 1 < NSL:
                wts[sl + 1] = load_w(sl + 1)
            w, bw = wts[sl]
            h, bh = pend
            if ji + 1 < len(jobs):
                pend = load_h(jobs[ji + 1])
            is_s5 = sl * 512 >= FW
            lat = blk < cfg.NB
            ntok = 512 if lat else 256
            for mc in range(4):
                ps, bps = pr.get()
                P.mm(ps[:, :ntok], [(w[:, kc, mc * 128:(mc + 1) * 128], h[:, kc, :ntok]) for kc in range(KC)],
                     [bw, bh], [bps])
                s_, bs_ = sr.get()
                col = sl * 512 + mc * 128
                ne += 1
                if not is_s5:
                    evac(P, ne, [bps], [bs_], s_[:], ps[:])
                    P.dma("gpsimd", T.zF[col // 128][:, blk * 512:(blk + 1) * 512], s_[:], reads=[bs_],
                          writes=[tb(T, "zF", (col // 128, blk))])
                else:
                    ch0 = col - FW
                    if lat:
                        evac(P, ne, [bps], [bs_], s_[:], ps[:])
                        P.dma("gpsimd", T.Dscr[ch0:ch0 + 128, blk * 4:(blk + 1) * 4, 0:128],
                              s_[:].rearrange("p (s r) -> p s r", s=4), reads=[bs_],
                              writes=[tb(T, "Dscr", ch0 // 128)], strict=False)
                    else:
                        evac(P, ne, [bps], [bs_], s_[:, 0:256].rearrange("p (s rc) -> p rc s", rc=4),
                             ps[:, 0:256].rearrange("p (rc s) -> p rc s", rc=4))
                        P.dma("gpsimd", T.Dscr[ch0:ch0 + 128, :, 128:132],
                              s_[:, 0:256].rearrange("p (s rc) -> p s rc", rc=4), reads=[bs_],
                              writes=[tb(T, "Dscr", ch0 // 128)], strict=False,
                              allow_slow_non_contiguous=True)
        precast(len(pre))


def build(cfg, upto=99, dbg=0):
    nc = bass.Bass("TRN2", target_bir_lowering=False)
    T = declare(nc, cfg, dbg)
    with ExitStack() as es:
        P = Prog(nc, es)
        KC = cfg.KC

        def persist(name, shape, dt):
            return es.enter_context(nc.sbuf_tensor(name, list(shape), dt)), Buf()
        per = {
            "modT": persist("modT", [128, 2, 6, KC], F32),
            "gs": persist("gs", [128, 3, KC], F32),
            "ident": persist("ident", [128, 128], BF16),
        }
        per["bmodB"] = Buf()
        per["bgsB"] = Buf()
        P.dma("sync", per["ident"][0][:], T.identb, writes=[per["ident"][1]])
        for i, st in enumerate(STAGES):
            if i > upto:
                break
            if st.__name__ in PER_STAGES:
                st(P, cfg, T, per)
            else:
                st(P, cfg, T)
        P.emit()
    return nc


PER_STAGES = set()
STAGES = [stage_adaln, stage_norm1, stage_win]


def dft_tables(cfg, q):
    L = cfg.L
    s = np.arange(64)[:, None]
    rt = np.arange(128)[None, :]
    n = (64 * ((rt + 32 * q) % 128) + s).astype(np.int64)
    ks = np.arange(64)[:, None]
    kr = np.arange(32)[None, :]
    k = (64 * (kr + 32 * q) + ks).reshape(-1).astype(np.int64)
    ang = (2.0 * np.pi / L) * ((n.reshape(-1)[:, None] * k[None, :]) % L).astype(np.float64)
    norm = 1.0 / math.sqrt(L * cfg.HD)
    tab = np.stack([np.cos(ang) * norm, -np.sin(ang) * norm], axis=1)
    tab = tab.reshape(64, 128, 2, 4, 512).transpose(3, 0, 1, 2, 4)
    return bf16(tab)


def chan_dft(cfg):
    j = np.arange(cfg.HD)
    ang = 2.0 * np.pi * ((j[:, None] * j[None, :]) % cfg.HD) / cfg.HD
    return np.concatenate([np.cos(ang), np.sin(ang)], axis=1).astype(np.float32)


def s5_consts(cfg, q):
    return np.zeros((128, S5C_N), np.float32)


def perm_rows(w):
    n = w.shape[0] // 128
    return np.ascontiguousarray(w.reshape(n, 8, 16, -1).transpose(0, 2, 1, 3).reshape(w.shape))


def make_in_map(cfg, inp, core):
    b, q = core // 4, core % 4
    D = cfg.D
    f = lambda a: np.ascontiguousarray(np.asarray(a, dtype=np.float32))
    xg = f(inp["x"][b]).reshape(128, 64, D)
    xrot = np.roll(xg, -32 * q, axis=0)
    xown = np.ascontiguousarray(xrot[:32].transpose(1, 0, 2)).reshape(cfg.NK, D)
    G = cfg.G
    m = {
        "xb": np.ascontiguousarray(xrot.reshape(cfg.L, D)),
        "xctx": f(inp["ctx"][b]),
        "xown": xown,
        "cvec": np.stack([f(inp["c"][b]), f(inp["c_ctx"])]),
        "ada_w": f(inp["ada_w"][0]), "ada_b": f(inp["ada_b"][0]).reshape(1, -1),
        "norm1_g": f(inp["norm1_g"][0]).reshape(1, -1), "norm2_g": f(inp["norm2_g"][0]).reshape(1, -1),
        "final_g": f(inp["final_g"]).reshape(1, -1),
        "w_in": f(inp["w_in"][0]), "w_out": f(inp["w_out"][0]),
        "fourier_w": f(inp["fourier_w"][0]).reshape(cfg.FW, cfg.HD),
        "lam_re": f(inp["s5_lam_re"][0]).reshape(2 * G, 64), "lam_im": f(inp["s5_lam_im"][0]).reshape(2 * G, 64),
        "log_dt": f(inp["s5_log_dt"][0]).reshape(1, 2 * G),
        "b_re": f(inp["s5_b_re"][0]).reshape(2 * G * 64, 16), "b_im": f(inp["s5_b_im"][0]).reshape(2 * G * 64, 16),
        "c_re": f(inp["s5_c_re"][0]).reshape(2 * G * 16, 64), "c_im": f(inp["s5_c_im"][0]).reshape(2 * G * 16, 64),
        "s5_d": f(inp["s5_d"][0]).reshape(1, -1),
        "glu_w_a": perm_rows(f(inp["glu_w_a"][0])), "glu_b_a": f(inp["glu_b_a"][0]).reshape(1, -1),
        "glu_w_b": perm_rows(f(inp["glu_w_b"][0])), "glu_b_b": f(inp["glu_b_b"][0]).reshape(1, -1),
        "ffn_w_gate": f(inp["ffn_w_gate"][0]), "ffn_w_up": f(inp["ffn_w_up"][0]),
        "ffn_w_down": f(inp["ffn_w_down"][0]),
        "cdft": chan_dft(cfg), "tdft": dft_tables(cfg, q),
        "identb": bf16(np.eye(128)), "s5c": s5_consts(cfg, q),
    }
    return m


def stage_fourier_ab(P, cfg, T):
    HD, HC, NH, FW, FCH = cfg.HD, cfg.HC, cfg.NH, cfg.FW, cfg.FCH
    with Phase(P) as ph:
        cd, bcd = ph.sb("cd", [128, HC, 2 * HD], F32)
        P.dma("sync", cd[:], T.cdft.rearrange("(c p) n -> p c n", p=128), writes=[bcd])
        wfr = ph.ring("wf", 2, [128, HC, HD], F32)
        pq, bpq = ph.sb("pq", [128, NH, 2, HC, HD], BF16)
        aps = ph.ring("abps", 6, [128, 512], F32, psum=True)
        ne = 0
        for h in range(NH):
            wf, bwf = wfr.get()
            P.dma("sync", wf[:], T.fourier_w[h * HD:(h + 1) * HD].rearrange("(c p) e -> p c e", p=128),
                  writes=[bwf])
            for ab in range(2):
                for jc in range(HC):
                    ps, bps = aps.get()
                    P.mm(ps[:, :HD], [(cd[:, kc, ab * HD + jc * 128: ab * HD + (jc + 1) * 128], wf[:, kc, :])
                                      for kc in range(HC)], [bcd, bwf], [bps])
                    ne += 1
                    evac(P, ne, [bps], [bpq], pq[:, h, ab, jc, :], ps[:, :HD], strict=False)
        zr = ph.ring("zFl", 2, [128, FCH, 512], BF16)
        abst = ph.ring("abst", 2, [128, 2, FW], BF16)
        zv = T.zF.rearrange("c p t -> p c t")
        for blk in range(cfg.NB):
            z, bz = zr.get()
            P.dma("sync", z[:], zv[:, :, blk * 512:(blk + 1) * 512],
                  reads=[tb(T, "zF", (c, blk)) for c in range(FCH)], writes=[bz])
            for ti in range(4):
                s = blk * 4 + ti
                ab_t, bab = abst.get()
                for h in range(NH):
                    for ab in range(2):
                        ps, bps = aps.get()
                        P.mm(ps[:, :HD], [(z[:, h * HC + c, ti * 128:(ti + 1) * 128], pq[:, h, ab, c, :])
                                          for c in range(HC)], [bz, bpq], [bps])
                        ne += 1
                        evac(P, ne, [bps], [bab], ab_t[:, ab, h * HD:(h + 1) * HD], ps[:, :HD], strict=False)
                P.dma("gpsimd", T.ABs[s], ab_t[:], reads=[bab], writes=[tb(T, "ABs", s)])


def stage_dft(P, cfg, T):
    FCH = cfg.FCH
    CG = min(8, FCH)
    with Phase(P) as ph:
        abr = ph.ring("ab", 3, [128, 2, CG * 128], BF16)
        tr = ph.ring("tab", 3, [128, 2, 512], BF16)
        psb = [ph.ps("dps", [128, 512]) for _ in range(CG)]
        yst = ph.ring("yst", 3, [128, 512], BF16)
        ne = 0
        for cg in range(FCH // CG):
            for kb in range(4):
                for n_ in range(64):
                    a, ba = abr.get()
                    t, bt = tr.get()
                    P.dma("sync", a[:], T.ABs[n_][:, :, cg * CG * 128:(cg + 1) * CG * 128],
                          reads=[tb(T, "ABs", n_)], writes=[ba])
                    P.dma("sync", t[:], T.tdft[kb, n_], writes=[bt])

                    def fn(e, a=a, t=t, n_=n_):
                        ins = None
                        for ch in range(CG):
                            e.matmul(psb[ch][0][:], a[:, 0, ch * 128:(ch + 1) * 128], t[:, 0, :],
                                     start=(n_ == 0), stop=False)
                            ins = e.matmul(psb[ch][0][:], a[:, 1, ch * 128:(ch + 1) * 128], t[:, 1, :],
                                           start=False, stop=(n_ == 63))
                        return ins
                    P.op("tensor", fn, [ba, bt], [psb[ch][1] for ch in range(CG)])
                for ch in range(CG):
                    y, by = yst.get()
                    ne += 1
                    evac(P, ne, [psb[ch][1]], [by], y[:], psb[ch][0][:])
                    P.dma("gpsimd", T.ycat[cg * CG + ch][:, kb * 512:(kb + 1) * 512], y[:], reads=[by],
                          writes=[tb(T, "ycat", (cg * CG + ch, kb))])


def stage_glu(P, cfg, T):
    SW, SCH, FCH, NK = cfg.SW, cfg.SW // 128, cfg.FCH, cfg.NK
    with Phase(P) as ph:
        g, bg = ph.sb("gT", [128, SCH, NK], BF16)
        P.dma("sync", g[:], T.Ys.rearrange("c p k -> p c k"), reads=[tb(T, "Ys", c) for c in range(SCH)],
              writes=[bg])
        bias, bbias = ph.sb("glub", [128, 2, SCH], F32)
        P.dma("sync", bias[:, 0], T.glu_b_a[0].rearrange("(c p) -> p c", p=128), writes=[bbias], strict=False,
              allow_slow_non_contiguous=True)
        P.dma("sync", bias[:, 1], T.glu_b_b[0].rearrange("(c p) -> p c", p=128), writes=[bbias], strict=False,
              allow_slow_non_contiguous=True)
        wr = ph.ring("gluw", 2, [128, 2, SCH, 256], BF16)
        pa = ph.ring("glupa", 2, [128, 512], F32, psum=True)
        pb = ph.ring("glupb", 2, [128, 512], F32, psum=True)
        sg = ph.ring("glusg", 2, [128, 512], F32)
        ost = ph.ring("gluo", 3, [128, 512], BF16)
        wa = T.glu_w_a.rearrange("(c p) n -> p c n", p=128)
        wb = T.glu_w_b.rearrange("(c p) n -> p c n", p=128)
        for sl in range(SW // 256):
            w, bw = wr.get()
            P.dma("gpsimd", w[:, 0], wa[:, :, sl * 256:(sl + 1) * 256], writes=[bw], strict=False)
            P.dma("gpsimd", w[:, 1], wb[:, :, sl * 256:(sl + 1) * 256], writes=[bw], strict=False)
            for mc in range(2):
                ch = sl * 2 + mc
                for kb in range(NK // 512):
                    a_ps, ba = pa.get()
                    b_ps, bb = pb.get()
                    rhs = lambda kc: g[:, kc, kb * 512:(kb + 1) * 512]
                    P.mm(a_ps[:], [(w[:, 0, kc, mc * 128:(mc + 1) * 128], rhs(kc)) for kc in range(SCH)],
                         [bw, bg], [ba])
                    P.mm(b_ps[:], [(w[:, 1, kc, mc * 128:(mc + 1) * 128], rhs(kc)) for kc in range(SCH)],
                         [bw, bg], [bb])
                    s_, bs_ = sg.get()
                    P.I("scalar", "activation", [bb, bbias], [bs_], out=s_[:], in_=b_ps[:], func=AF.Sigmoid,
                        bias=bias[:, 1, ch:ch + 1], scale=1.0)
                    o, bo = ost.get()
                    P.I("vector", "scalar_tensor_tensor", [ba, bs_, bbias], [bo], out=o[:], in0=a_ps[:],
                        scalar=bias[:, 0, ch:ch + 1], in1=s_[:], op0=ALU.add, op1=ALU.mult)
                    P.dma("sync", T.ycat[FCH + ch][:, kb * 512:(kb + 1) * 512], o[:], reads=[bo],
                          writes=[tb(T, "ycat", (FCH + ch, kb))])


def gemm_tm(P, ph, cfg, act, bact, nK, W, wname, T, wring, psb, epilogue):
    D = cfg.D
    for ds in range(D // 1024):
        for kc in range(nK):
            w, bw = wring.get()
            P.dma("gpsimd", w[:], W[kc * 128:(kc + 1) * 128, ds * 1024:(ds + 1) * 1024],
                  reads=([tb(T, wname, kc)] if PRECAST else []), writes=[bw])

            def fn(e, w=w, kc=kc):
                ins = None
                for ti in range(4):
                    for hf in range(2):
                        ins = e.matmul(psb[ti * 2 + hf][0][:], act[:, kc, ti * 128:(ti + 1) * 128],
                                       w[:, hf * 512:(hf + 1) * 512], start=(kc == 0), stop=(kc == nK - 1))
                return ins
            P.op("tensor", fn, [bact, bw], [p[1] for p in psb])
        for ti in range(4):
            epilogue(ds, ti, [psb[ti * 2][0], psb[ti * 2 + 1][0]], [psb[ti * 2][1], psb[ti * 2 + 1][1]])


def stage_back(P, cfg, T, per):
    D, KC, FH, FC, NK = cfg.D, cfg.KC, cfg.FH, cfg.FC, cfg.NK
    modT = per["modT"][0]
    gs = per["gs"][0]
    bmod, bgs = per["bmodB"], per["bgsB"]
    with Phase(P) as ph:
        aT, baT = ph.sb("aT", [128, max(FC, KC), 512], BF16)
        h2T, bh2 = ph.sb("h2T", [128, KC, 512], BF16)
        wring = ph.ring("wtm", 6, [128, 1024], BF16)
        gur = ph.ring("guw", 2, [128, 2, KC, 128], BF16)
        psb = [ph.ps("bps", [128, 512]) for _ in range(8)]
        xr = ph.ring("xc", 2, [128, 1024], F32)
        gr = ph.ring("gc", 2, [128, 1024], F32)
        tr = ph.ring("tc", 3, [128, 1024], F32)
        xnr = ph.ring("xnc", 2, [128, 1024], BF16)
        junk = ph.sb("junkb", [128, 1024], BF16)
        sgr = ph.ring("sgr", 2, [128, 512], F32)
        tps = Ring([(psb[i][0][:].bitcast(BF16).rearrange("p (a b) -> p a b", a=8), psb[i][1]) for i in (6, 7)])
        ycv = T.ycat.rearrange("c p k -> p c k")
        wg = T.ffn_w_gate.rearrange("(c p) n -> p c n", p=128)
        wu = T.ffn_w_up.rearrange("(c p) n -> p c n", p=128)
        for kb in range(NK // 512):
            st, bst = ph.sb("bst", [128, 4, 8], F32)
            P.I("vector", "memset", [], [bst], st[:], 0.0)
            P.dma("sync", aT[:, 0:KC, :], ycv[:, :, kb * 512:(kb + 1) * 512],
                  reads=[tb(T, "ycat", (c, kb)) for c in range(KC)], writes=[baT])

            def epi1(ds, ti, pst, bpst, kb=kb, st=st, bst=bst):
                tile = kb * 4 + ti
                x, bx = xr.get()
                P.dma("sync", x[:], T.xown[tile * 128:(tile + 1) * 128, ds * 1024:(ds + 1) * 1024], writes=[bx])
                gt, bg_ = gr.get()
                P.dma("sync", gt[:], row_bcast(T.modscr[0:1, 2 * D + ds * 1024: 2 * D + (ds + 1) * 1024], 128),
                      reads=[tb(T, "modscr")], writes=[bg_])
                t_, bt_ = tr.get()
                for hf in range(2):
                    P.I("vector", "tensor_tensor", [bpst[hf], bg_], [bt_], out=t_[:, hf * 512:(hf + 1) * 512],
                        in0=pst[hf][:], in1=gt[:, hf * 512:(hf + 1) * 512], op=ALU.mult, strict=False)
                P.I("gpsimd", "tensor_tensor", [bx], [bt_], out=t_[:], in0=t_[:], in1=x[:], op=ALU.add)
                P.I("scalar", "activation", [bt_], [junk[1], bst], out=junk[0][:], in_=t_[:], func=AF.Square,
                    accum_out=st[:, ti, ds:ds + 1], strict=False)
                P.dma("sync", T.X1[tile][:, ds * 1024:(ds + 1) * 1024], t_[:], reads=[bt_],
                      writes=[tb(T, "X1", tile)], strict=False)
            gemm_tm(P, ph, cfg, aT, baT, KC, T.WoB if PRECAST else T.w_out, "WoB", T, wring, psb, epi1)
            for ti in range(4):
                tile = kb * 4 + ti
                P.I("vector", "tensor_reduce", [bst], [bst], out=st[:, ti, 4:5], in_=st[:, ti, 0:D // 1024],
                    axis=AX.X, op=ALU.add)
                P.I("vector", "tensor_scalar", [bst], [bst], out=st[:, ti, 5:6], in0=st[:, ti, 4:5],
                    scalar1=1.0 / D, scalar2=EPS, op0=ALU.mult, op1=ALU.add)
                P.I("scalar", "activation", [bst], [bst], out=st[:, ti, 6:7], in_=st[:, ti, 5:6], func=AF.Sqrt)
                P.I("vector", "reciprocal", [bst], [bst], out=st[:, ti, 7:8], in_=st[:, ti, 6:7])
                for ds in range(D // 1024):
                    x1, bx1 = xr.get()
                    P.dma("sync", x1[:], T.X1[tile][:, ds * 1024:(ds + 1) * 1024], reads=[tb(T, "X1", tile)],
                          writes=[bx1])
                    xn, bxn = xnr.get()
                    P.I("vector", "tensor_scalar", [bx1, bst], [bxn], out=xn[:], in0=x1[:],
                        scalar1=st[:, ti, 7:8], scalar2=None, op0=ALU.mult)
                    transpose_mod(P, xn, bxn, 0, 8, per["ident"], tps, h2T, bh2, ti * 128, 128,
                                  gs[:, 2], modT[:, 0, 3], [bmod, bgs], ds * 8)
            for f in range(FC):
                w, bw = gur.get()
                P.dma("gpsimd", w[:, 0], wg[:, :, f * 128:(f + 1) * 128], writes=[bw], strict=False)
                P.dma("gpsimd", w[:, 1], wu[:, :, f * 128:(f + 1) * 128], writes=[bw], strict=False)
                pg, bpg = psb[(f % 4) * 2]
                pu, bpu = psb[(f % 4) * 2 + 1]
                P.mm(pg[:], [(w[:, 0, kc, :], h2T[:, kc, :]) for kc in range(KC)], [bw, bh2], [bpg])
                P.mm(pu[:], [(w[:, 1, kc, :], h2T[:, kc, :]) for kc in range(KC)], [bw, bh2], [bpu])
                s_, bs_ = sgr.get()
                P.I("scalar", "activation", [bpg], [bs_], out=s_[:], in_=pg[:], func=AF.Silu)
                P.I("vector", "tensor_tensor", [bpu, bs_], [baT], out=aT[:, f, :], in0=pu[:], in1=s_[:],
                    op=ALU.mult, strict=False)

            def epi2(ds, ti, pst, bpst, kb=kb):
                tile = kb * 4 + ti
                x, bx = xr.get()
                P.dma("sync", x[:], T.X1[tile][:, ds * 1024:(ds + 1) * 1024], reads=[tb(T, "X1", tile)],
                      writes=[bx])
                gt, bg_ = gr.get()
                P.dma("sync", gt[:], row_bcast(T.modscr[0:1, 5 * D + ds * 1024: 5 * D + (ds + 1) * 1024], 128),
                      reads=[tb(T, "modscr")], writes=[bg_])
                t_, bt_ = tr.get()
                for hf in range(2):
                    P.I("vector", "tensor_tensor", [bpst[hf], bg_], [bt_], out=t_[:, hf * 512:(hf + 1) * 512],
                        in0=pst[hf][:], in1=gt[:, hf * 512:(hf + 1) * 512], op=ALU.mult, strict=False)
                P.I("gpsimd", "tensor_tensor", [bx], [bt_], out=t_[:], in0=t_[:], in1=x[:], op=ALU.add)
                P.dma("sync", T.X2[tile][:, ds * 1024:(ds + 1) * 1024], t_[:], reads=[bt_],
                      writes=[tb(T, "X2", tile)], strict=False)
            gemm_tm(P, ph, cfg, aT, baT, FC, T.WdB if PRECAST else T.ffn_w_down, "WdB", T, wring, psb, epi2)


def stage_final(P, cfg, T):
    D = cfg.D
    with Phase(P) as ph:
        fg, bfg = ph.sb("fg", [128, D], F32)
        P.dma("sync", fg[:], row_bcast(T.final_g[0:1, :], 128), writes=[bfg])
        xr = ph.ring("x2", 2, [128, D], F32)
        orr = ph.ring("o", 2, [128, D], F32)
        junk = ph.sb("junkf", [128, D], BF16)
        st = ph.ring("stf", 4, [128, 4], F32)
        for tile in range(16):
            x, bx = xr.get()
            P.dma("sync", x[:], T.X2[tile], reads=[tb(T, "X2", tile)], writes=[bx])
            rstd, bst = norm_tile(P, ph, x[:], bx, D, st, junk)
            o, bo = orr.get()
            P.I("vector", "scalar_tensor_tensor", [bx, bst, bfg], [bo], out=o[:], in0=x[:], scalar=rstd,
                in1=fg[:], op0=ALU.mult, op1=ALU.mult)
            P.dma("gpsimd", T.out[tile * 128:(tile + 1) * 128, :], o[:], reads=[bo])


STAGES = [stage_adaln, stage_norm1, stage_win, stage_fourier_ab, stage_dft, stage_glu, stage_back, stage_final]
PER_STAGES = {"stage_back"}


NM = 1 + 5 + 32 + 132 + 1
OFF_EVP = 0
OFF_EVQ = OFF_EVP + 128
OFF_EVM = OFF_EVQ + 128
OFF_MSK = OFF_EVM + 2 * NM
OFF_SGN = OFF_MSK + 2 * 132
OFF_MF = OFF_SGN + 4
OFF_MB = OFF_MF + 128
OFF_ID = OFF_MB + 128
S5C_N = OFF_ID + 128


def s5_consts(cfg, q):
    t = np.zeros((128, S5C_N), np.float64)
    s = np.arange(64)
    t[:, OFF_EVP:OFF_EVP + 64] = -s
    t[:, OFF_EVP + 64:OFF_EVP + 128] = s
    t[:, OFF_EVP + 128:OFF_EVP + 192] = -(63 - s)
    t[:, OFF_EVP + 192:OFF_EVP + 256] = 63 - s
    R0 = 32 * q
    for d in range(2):
        ev = np.zeros(NM)
        msk = np.zeros(132)
        ev[0] = 1
        ev[1:6] = 64 * 2 ** np.arange(5)
        rho = np.arange(32)
        ev[6:38] = 64 * (rho if d == 0 else 31 - rho)
        ev[NM - 1] = 63
        for rp in range(132):
            if rp >= 128:
                rc = rp - 128
                idx = rc if d == 0 else 3 - rc
            else:
                r = (rp + R0) % 128
                idx = 4 + r if d == 0 else 4 + 127 - r
            enter = 4 + R0 if d == 0 else 4 + 127 - (R0 + 31)
            k = enter - idx - 1
            if k >= 0:
                ev[38 + rp] = 64 * k
                msk[rp] = 1.0
        t[:, OFF_EVM + d * NM:OFF_EVM + (d + 1) * NM] = ev
        t[:, OFF_MSK + d * 132:OFF_MSK + (d + 1) * 132] = msk
    hi = (np.arange(128) >= 64).astype(np.float64)
    t[:, OFF_SGN + 0] = 2 * hi - 1
    t[:, OFF_SGN + 1] = 1 - hi
    t[:, OFF_SGN + 2] = -hi
    t[:, OFF_SGN + 3] = -(1 - hi)
    sl = np.arange(128) // 16
    t[:, OFF_MF:OFF_MF + 128] = (sl[None, :] >= sl[:, None])
    t[:, OFF_MB:OFF_MB + 128] = (sl[:, None] >= sl[None, :])
    t[:, OFF_ID:OFF_ID + 128] = np.eye(128)
    return t.astype(np.float32)


def bc(ap, shape):
    return ap.broadcast_to(list(shape))


class S5:
    def __init__(self, P, cfg, T, ph):
        self.P, self.cfg, self.T = P, cfg, T
        G = cfg.G
        self.cst, self.bcst = ph.sb("s5c", [128, S5C_N], F32)
        P.dma("sync", self.cst[:], T.s5c, writes=[self.bcst])
        self.par, self.bpar = ph.sb("s5par", [128, 8, 2 * G], F32)
        par, bpar = self.par, self.bpar
        for h in range(2):
            for d in range(2):
                P.dma("sync", par[h * 64:(h + 1) * 64, 0, d * G:(d + 1) * G],
                      T.lam_re[d * G:(d + 1) * G].rearrange("g p -> p g"), writes=[bpar],
                      strict=False, allow_slow_non_contiguous=True)
                P.dma("sync", par[h * 64:(h + 1) * 64, 1, d * G:(d + 1) * G],
                      T.lam_im[d * G:(d + 1) * G].rearrange("g p -> p g"), writes=[bpar],
                      strict=False, allow_slow_non_contiguous=True)
        P.dma("sync", par[:, 2], row_bcast(T.log_dt[0:1, :], 128), writes=[bpar], strict=False)
        V = lambda *a, **k: P.I("vector", *a, **k)
        V("tensor_scalar_min", [bpar], [bpar], out=par[:, 0], in0=par[:, 0], scalar1=-1e-4)
        P.I("scalar", "activation", [bpar], [bpar], out=par[:, 2], in_=par[:, 2], func=AF.Exp)
        V("tensor_tensor", [bpar], [bpar], out=par[:, 3], in0=par[:, 0], in1=par[:, 2], op=ALU.mult)
        V("tensor_tensor", [bpar], [bpar], out=par[:, 4], in0=par[:, 1], in1=par[:, 2], op=ALU.mult)
        self.dcol, self.bdcol = ph.sb("dcol", [128, G], F32)
        for sl in range(8):
            P.dma("sync", self.dcol[sl * 16:(sl + 1) * 16, :], T.s5_d[0].rearrange("(g i) -> i g", i=16),
                  writes=[self.bdcol], strict=False, allow_slow_non_contiguous=True)

    def powers(self, ph, a, th, ev, n1, n, tag):
        P = self.P
        shp = [128, n1, n]
        ang, b0 = ph.sb("ang" + tag, shp, F32)
        w1, b1 = ph.sb("pw1" + tag, shp, F32)
        w2, b2 = ph.sb("pw2" + tag, shp, F32)
        ere, bre = ph.sb("ere" + tag, shp, F32)
        eim, bim = ph.sb("eim" + tag, shp, F32)
        rd = [self.bpar, self.bcst]
        a3 = bc(a.unsqueeze(2), shp)
        t3 = bc(th.unsqueeze(2), shp)
        P.I("vector", "tensor_tensor", rd, [b0], out=ang[:], in0=t3, in1=ev, op=ALU.mult)
        P.I("gpsimd", "tensor_tensor", rd, [bre], out=ere[:], in0=a3, in1=ev, op=ALU.mult)
        P.I("scalar", "activation", [bre], [bre], out=ere[:], in_=ere[:], func=AF.Exp)
        for (shift, dst, bdst) in ((0.0, w1, b1), (0.25, w2, b2)):
            P.I("vector", "tensor_scalar", [b0], [bdst], out=dst[:], in0=ang[:], scalar1=1.0 / TWO_PI,
                scalar2=shift, op0=ALU.mult, op1=ALU.add)
            P.I("vector", "tensor_scalar_add", [bdst], [bim], out=eim[:], in0=dst[:], scalar1=MAGIC)
            P.I("vector", "tensor_scalar_add", [bim], [bim], out=eim[:], in0=eim[:], scalar1=-MAGIC)
            P.I("vector", "tensor_tensor", [bdst, bim], [bdst], out=dst[:], in0=dst[:], in1=eim[:],
                op=ALU.subtract)
            P.I("scalar", "activation", [bdst], [bdst], out=dst[:], in_=dst[:], func=AF.Sin, scale=TWO_PI)
        P.I("vector", "tensor_tensor", [bre, b1], [bim], out=eim[:], in0=ere[:], in1=w1[:], op=ALU.mult)
        P.I("gpsimd", "tensor_tensor", [bre, b2], [bre], out=ere[:], in0=ere[:], in1=w2[:], op=ALU.mult)
        return (ere, bre), (eim, bim)

    def tmp(self, ph, key, shp):
        cache = ph.__dict__.setdefault("_tmpc", {})
        n = int(np.prod(shp[1:]))
        if key not in cache:
            cache[key] = ph.sb("tmp" + key, [128, max(n, getattr(ph, "_tmpn", n))], F32)
        t, b = cache[key]
        v = t[0:shp[0], 0:n]
        if len(shp) == 3:
            v = v.rearrange("p (a b) -> p a b", a=shp[1])
        elif len(shp) == 4:
            v = v.rearrange("p (a b c) -> p a b c", a=shp[1], b=shp[2])
        return v, b

    def cmul(self, ph, ore, bore, oim, boim, are, aim, ba, bre_, bim_, bb, shp, tag):
        P = self.P
        t1, bt1 = self.tmp(ph, "c1", shp)
        t2, bt2 = self.tmp(ph, "c2", shp)
        P.I("vector", "tensor_tensor", ba + bb, [bt1], out=t1, in0=aim, in1=bim_, op=ALU.mult)
        P.I("gpsimd", "tensor_tensor", ba + bb, [bt2], out=t2, in0=aim, in1=bre_, op=ALU.mult)
        P.I("vector", "tensor_tensor", ba + bb, [bore], out=ore, in0=are, in1=bre_, op=ALU.mult)
        P.I("gpsimd", "tensor_tensor", ba + bb, [boim], out=oim, in0=are, in1=bim_, op=ALU.mult)
        P.I("vector", "tensor_tensor", [bt1], [bore], out=ore, in0=ore, in1=t1, op=ALU.subtract)
        P.I("gpsimd", "tensor_tensor", [bt2], [boim], out=oim, in0=oim, in1=t2, op=ALU.add)

    def outer2(self, ph, out, bout, x1, x2, bx, c1, c2, bcb, n1, tag):
        P = self.P
        H = max(1, n1 // 4)
        shp = [128, H, 64, 16]
        ta, bta = ph.sb("o2a" + tag, shp, F32)
        tb_, btb = ph.sb("o2b" + tag, shp, F32)
        for k in range(0, n1, H):
            X1 = bc(x1[:, k:k + H].unsqueeze(3), shp)
            X2 = bc(x2[:, k:k + H].unsqueeze(3), shp)
            C1 = bc(c1[:, k:k + H].unsqueeze(2), shp)
            C2 = bc(c2[:, k:k + H].unsqueeze(2), shp)
            P.I("vector", "tensor_tensor", bx + bcb, [bta], out=ta[:], in0=X1, in1=C1, op=ALU.mult)
            P.I("gpsimd", "tensor_tensor", bx + bcb, [btb], out=tb_[:], in0=X2, in1=C2, op=ALU.mult)
            P.I("vector", "tensor_tensor", [bta, btb], [bout], out=out[:, k:k + H], in0=ta[:], in1=tb_[:],
                op=ALU.add, strict=False)


def stage_s5(P, cfg, T):
    G, GT, RP = cfg.G, cfg.GT, cfg.RP
    GELU_C = 2.0 * math.sqrt(2.0 / math.pi)
    with Phase(P) as ph0:
        S = S5(P, cfg, T, ph0)
        cst, bcst, par, bpar = S.cst, S.bcst, S.par, S.bpar
        identf = cst[:, OFF_ID:OFF_ID + 128]
        ident_b, bident_b = ph0.sb("identb2", [128, 128], BF16)
        P.dma("sync", ident_b[:], T.identb, writes=[bident_b])
        with Phase(P) as ph:
            ev1 = bc(cst[:, OFF_EVM:OFF_EVM + 1].unsqueeze(1), [128, 2 * G, 1])
            (e1r, be1r), (e1i, be1i) = S.powers(ph, par[:, 3], par[:, 4], ev1, 2 * G, 1, "k")
            nr, bnr = ph.sb("knr", [128, 2 * G], F32)
            den, bden = ph.sb("kden", [128, 2 * G], F32)
            t0, bt0 = ph.sb("kt0", [128, 2 * G], F32)
            V = lambda *a, **k: P.I("vector", *a, **k)
            V("tensor_scalar_add", [be1r], [bnr], out=nr[:], in0=e1r[:, :, 0], scalar1=-1.0)
            V("tensor_tensor", [bpar], [bden], out=den[:], in0=par[:, 0], in1=par[:, 0], op=ALU.mult)
            V("tensor_tensor", [bpar], [bt0], out=t0[:], in0=par[:, 1], in1=par[:, 1], op=ALU.mult)
            V("tensor_tensor", [bt0], [bden], out=den[:], in0=den[:], in1=t0[:], op=ALU.add)
            V("reciprocal", [bden], [bden], out=den[:], in_=den[:])
            V("tensor_tensor", [bnr, bpar], [bt0], out=t0[:], in0=nr[:], in1=par[:, 0], op=ALU.mult)
            V("tensor_tensor", [be1i, bpar], [bpar], out=par[:, 7], in0=e1i[:, :, 0], in1=par[:, 1], op=ALU.mult)
            V("tensor_tensor", [bt0, bpar], [bt0], out=t0[:], in0=t0[:], in1=par[:, 7], op=ALU.add)
            V("tensor_tensor", [bt0, bden], [bpar], out=par[:, 5], in0=t0[:], in1=den[:], op=ALU.mult)
            V("tensor_tensor", [be1i, bpar], [bt0], out=t0[:], in0=e1i[:, :, 0], in1=par[:, 0], op=ALU.mult)
            V("tensor_tensor", [bnr, bpar], [bpar], out=par[:, 7], in0=nr[:], in1=par[:, 1], op=ALU.mult)
            V("tensor_tensor", [bt0, bpar], [bt0], out=t0[:], in0=t0[:], in1=par[:, 7], op=ALU.subtract)
            V("tensor_tensor", [bt0, bden], [bpar], out=par[:, 6], in0=t0[:], in1=den[:], op=ALU.mult)
        pv = par[:].rearrange("p k (d g) -> p k d g", d=2)
        def adv(k):
            return None
        for gt in range(GT):
            s5_subbatch(P, cfg, T, S, gt, pv, ident_b, bident_b, identf, GELU_C, adv)


def s5_subbatch(P, cfg, T, S, gt, pv, ident_b, bident_b, identf, GELU_C, adv):
    G, RP = cfg.G, cfg.RP
    cst, bcst, par, bpar = S.cst, S.bcst, S.par, S.bpar
    g0 = gt * 8
    V = lambda *a, **k: P.I("vector", *a, **k)
    GP = lambda *a, **k: P.I("gpsimd", *a, **k)
    ACT = lambda *a, **k: P.I("scalar", *a, **k)

    def prm(k):
        return pv[:, k, :, g0:g0 + 8]

    def ev2(off, n, per_d):
        a = cst[:, off:off + 2 * per_d].rearrange("p (d n) -> p d n", d=2)[:, :, 0:n]
        return bc(a.unsqueeze(2), [128, 2, 8, n])

    with Phase(P) as ph:
        PQ, bPQ = ph.sb("PQ", [128, 16, 64, 16], BF16)
        Psb, bPsb = ph.sb("Psb", [128, 16, 8, 128], BF16)
        U, bU = ph.sb("U", [128, 8, 8, RP], BF16)
        Pt0, bPt0 = ph.sb("Pt0", [128, 16, 128], BF16)
        Z0p, bZ0p = ph.sb("Z0p", [128, 16, 32], F32)
        Dsb, bDsb = ph.sb("Dsb", [128, 8, 128], BF16)
        psr = Ring([ph.ps("s5ps", [128, 512], F32) for _ in range(8)])
        (er, ber), (ei, bei) = s5_pow(S, ph, prm, ev2(OFF_EVP, 128, 128), 128, "pq")
        dv = T.Dscr.rearrange("c (j sl) r -> c sl j r", sl=8)
        for gl in range(8):
            for sl in range(8):
                ch0 = (g0 + gl) * 16
                P.dma("sync", U[sl * 16:(sl + 1) * 16, gl], dv[ch0:ch0 + 16, sl],
                      reads=[tb(T, "Dscr", gt)], writes=[bU], strict=False)
        with Phase(P) as p1:
            p1._tmpn = 16 * 64
            Bx, bBx = p1.sb("Bx", [128, 2, 8, 16], F32)
            By, bBy = p1.sb("By", [128, 2, 8, 16], F32)
            brv = T.b_re.rearrange("(d g p) i -> p d g i", d=2, g=G)[:, :, g0:g0 + 8, :]
            biv = T.b_im.rearrange("(d g p) i -> p d g i", d=2, g=G)[:, :, g0:g0 + 8, :]
            for d in range(2):
                P.dma("sync", Bx[0:64, d], brv[:, d], writes=[bBx], strict=False)
                P.dma("sync", Bx[64:128, d], biv[:, d], writes=[bBx], strict=False)
                P.dma("sync", By[0:64, d], biv[:, d], writes=[bBy], strict=False)
                P.dma("sync", By[64:128, d], brv[:, d], writes=[bBy], strict=False)
            w1, bw1 = p1.sb("w1", [128, 2, 8, 64], F32)
            w2, bw2 = p1.sb("w2", [128, 2, 8, 64], F32)
            shp = [128, 2, 8, 64]
            S.cmul(p1, w1[:], bw1, w2[:], bw2, bc(prm(5).unsqueeze(3), shp), bc(prm(6).unsqueeze(3), shp),
                   [bpar], er[:, :, :, 0:64], ei[:, :, :, 0:64], [ber, bei], shp, "w")
            V("tensor_scalar", [bw2, bcst], [bw2], out=w2[:], in0=w2[:], scalar1=cst[:, OFF_SGN:OFF_SGN + 1],
              scalar2=None, op0=ALU.mult)
            S.outer2(p1, PQ[:], bPQ, w1[:].rearrange("p d g s -> p (d g) s"),
                     w2[:].rearrange("p d g s -> p (d g) s"), [bw1, bw2],
                     Bx[:].rearrange("p d g i -> p (d g) i"), By[:].rearrange("p d g i -> p (d g) i"),
                     [bBx, bBy], 16, "p")
        V("tensor_copy", [bPQ], [bPt0], out=Pt0[:], in_=PQ[:].rearrange("p a s i -> p a (s i)")[:, :, 0:128])
        PQf = PQ[:].rearrange("p a s i -> p a (s i)")
        for gd in range(16):
            ps, bps = psr.get()
            tpv = ps[:].bitcast(BF16).rearrange("p (a b) -> p a b", a=8)

            def fn(e, gd=gd, tpv=tpv):
                ins = None
                for j in range(8):
                    ins = e.transpose(tpv[:, j, :], PQf[:, gd, j * 128:(j + 1) * 128], ident_b[:])
                return ins
            P.op("tensor", fn, [bPQ, bident_b], [bps])
            evac(P, gd, [bps], [bPsb], Psb[:, gd], tpv, strict=False)
        with Phase(P) as p2:
            p2._tmpn = 16 * RP
            adv(30)
            tot, btot = p2.sb("tot", [64, 2, 16, RP], F32)
            for gd in range(16):
                ps, bps = psr.get()
                pvw = ps[0:64, 0:2 * RP].rearrange("p (c r) -> p c r", c=2)
                for c in range(2):
                    P.mm(pvw[:, c, :], [(Psb[:, gd, j, c * 64:(c + 1) * 64], U[:, gd % 8, j, :]) for j in range(8)],
                         [bPsb, bU], [bps], strict=False)
                evac(P, gd, [bps], [btot], tot[:, :, gd, :], pvw, strict=False)
            (mr, bmr), (mi, bmi) = s5_pow(S, p2, prm, ev2(OFF_EVM, NM, NM), NM, "m")
            mr64 = mr[0:64].rearrange("p d g n -> p (d g) n")
            mi64 = mi[0:64].rearrange("p d g n -> p (d g) n")
            Tr, bTr = p2.sb("Tr", [64, 16, RP], F32)
            Ti, bTi = p2.sb("Ti", [64, 16, RP], F32)
            shp = [64, 16, RP]
            l63r = bc(mr64[:, :, NM - 1:NM], shp)
            l63i = bc(mi64[:, :, NM - 1:NM], shp)
            S.cmul(p2, Tr[:], bTr, Ti[:], bTi, l63r, l63i, [bmr, bmi], tot[:, 0], tot[:, 1], [btot], shp, "t")
            mskv = bc(cst[0:64, OFF_MSK:OFF_MSK + 2 * RP].rearrange("p (d n) -> p d n", d=2).unsqueeze(2),
                      [64, 2, 8, RP])
            V("tensor_tensor", [bcst], [bmr], out=mr[0:64, :, :, 38:38 + RP], in0=mr[0:64, :, :, 38:38 + RP],
              in1=mskv, op=ALU.mult)
            GP("tensor_tensor", [bcst], [bmi], out=mi[0:64, :, :, 38:38 + RP], in0=mi[0:64, :, :, 38:38 + RP],
               in1=mskv, op=ALU.mult)
            bHr, bHi = Buf(), Buf()
            S.cmul(p2, tot[:, 0], btot, tot[:, 1], btot, mr64[:, :, 38:38 + RP], mi64[:, :, 38:38 + RP],
                   [bmr, bmi], Tr[:], Ti[:], [bTr, bTi], shp, "h")
            hin, bhin = p2.sb("hin", [64, 2, 16], F32)
            V("tensor_reduce", [btot], [bhin], out=hin[:, 0], in_=tot[:, 0], axis=AX.X, op=ALU.add, strict=False)
            V("tensor_reduce", [btot], [bhin], out=hin[:, 1], in_=tot[:, 1], axis=AX.X, op=ALU.add, strict=False)
            yr, byr = p2.sb("yr", [64, 16, 32], F32)
            yi, byi = p2.sb("yi", [64, 16, 32], F32)
            V("tensor_copy", [bTr], [byr], out=yr[:], in_=Tr[:, :, 0:32])
            GP("tensor_copy", [bTi], [byi], out=yi[:], in_=Ti[:, :, 0:32])
            ur, bur = p2.sb("ur", [64, 16, 32], F32)
            ui, bui = p2.sb("ui", [64, 16, 32], F32)
            for m in range(5):
                sh = 1 << m
                n = 32 - sh
                for d in range(2):
                    gsl = slice(d * 8, (d + 1) * 8)
                    src = slice(0, n) if d == 0 else slice(sh, 32)
                    dst = slice(sh, 32) if d == 0 else slice(0, n)
                    shp2 = [64, 8, n]
                    ar = bc(mr64[:, gsl, 1 + m:2 + m], shp2)
                    ai = bc(mi64[:, gsl, 1 + m:2 + m], shp2)
                    S.cmul(p2, ur[:, gsl, 0:n], bur, ui[:, gsl, 0:n], bui, ar, ai, [bmr, bmi],
                           yr[:, gsl, src], yi[:, gsl, src], [byr, byi], shp2, f"s{m}{d}")
                    V("tensor_tensor", [bur], [byr], out=yr[:, gsl, dst], in0=yr[:, gsl, dst], in1=ur[:, gsl, 0:n],
                      op=ALU.add)
                    GP("tensor_tensor", [bui], [byi], out=yi[:, gsl, dst], in0=yi[:, gsl, dst], in1=ui[:, gsl, 0:n],
                       op=ALU.add)
            er_, ber_, ei_, bei_ = ur, bur, ui, bui
            shp3 = [64, 16, 32]
            S.cmul(p2, er_[:], ber_, ei_[:], bei_, mr64[:, :, 6:38], mi64[:, :, 6:38], [bmr, bmi],
                   bc(hin[:, 0].unsqueeze(2), shp3), bc(hin[:, 1].unsqueeze(2), shp3), [bhin], shp3, "e")
            for d in range(2):
                gsl = slice(d * 8, (d + 1) * 8)
                src = slice(0, 31) if d == 0 else slice(1, 32)
                dst = slice(1, 32) if d == 0 else slice(0, 31)
                V("tensor_tensor", [byr], [ber_], out=er_[:, gsl, dst], in0=er_[:, gsl, dst], in1=yr[:, gsl, src],
                  op=ALU.add)
                GP("tensor_tensor", [byi], [bei_], out=ei_[:, gsl, dst], in0=ei_[:, gsl, dst], in1=yi[:, gsl, src],
                   op=ALU.add)
            z0r, bz0r = p2.sb("z0r", [64, 16, 32], F32)
            z0i, bz0i = p2.sb("z0i", [64, 16, 32], F32)
            S.cmul(p2, z0r[:], bz0r, z0i[:], bz0i, bc(mr64[:, :, 0:1], shp3), bc(mi64[:, :, 0:1], shp3),
                   [bmr, bmi], er_[:], ei_[:], [ber_, bei_], shp3, "z")
            P.dma("sync", Z0p[0:64], z0r[:], reads=[bz0r], writes=[bZ0p], strict=False)
            P.dma("sync", Z0p[64:128], z0i[:], reads=[bz0i], writes=[bZ0p], strict=False)
        adv(15)
        with Phase(P) as p3:
            Cx, bCx = p3.sb("Cx", [128, 2, 8, 16], F32)
            Cy, bCy = p3.sb("Cy", [128, 2, 8, 16], F32)
            cl, bcl = p3.sb("cl", [128, 2, 2, 2, 64], F32)
            for d in range(2):
                r0 = (d * G + g0) * 16
                for ri, src in enumerate((T.c_re, T.c_im)):
                    for h in range(2):
                        P.dma("sync", cl[:, d, ri, h], src[r0:r0 + 128, :], writes=[bcl], strict=False)
            for d in range(2):
                for ri, (dst, bdst) in enumerate(((Cx, bCx), (Cy, bCy))):
                    ps, bps = psr.get()
                    P.op("tensor", lambda e, ps=ps, d=d, ri=ri: e.transpose(
                        ps[:, 0:128], cl[:, d, ri].rearrange("p h q -> p (h q)"), identf), [bcl, bcst], [bps])
                    evac(P, d * 2 + ri, [bps], [bdst], dst[:, d].rearrange("p g o -> p (g o)"), ps[:, 0:128],
                         strict=False)
            qr, bqr, qi, bqi = er[:, :, :, 64:128], ber, ei[:, :, :, 64:128], bei
            x1, bx1 = p3.sb("x1", [128, 2, 8, 64], F32)
            x2, bx2 = p3.sb("x2", [128, 2, 8, 64], F32)
            sc = lambda k: cst[:, OFF_SGN + k:OFF_SGN + k + 1]
            V("tensor_scalar", [bqr, bcst], [bx1], out=x1[:], in0=qr, scalar1=sc(1), scalar2=None, op0=ALU.mult)
            V("scalar_tensor_tensor", [bqi, bcst, bx1], [bx1], out=x1[:], in0=qi, scalar=sc(2), in1=x1[:],
              op0=ALU.mult, op1=ALU.add)
            GP("tensor_scalar", [bqi, bcst], [bx2], out=x2[:], in0=qi, scalar1=sc(3), scalar2=None, op0=ALU.mult)
            V("scalar_tensor_tensor", [bqr, bcst, bx2], [bx2], out=x2[:], in0=qr, scalar=sc(2), in1=x2[:],
              op0=ALU.mult, op1=ALU.add)
            S.outer2(p3, PQ[:], bPQ, x1[:].rearrange("p d g s -> p (d g) s"),
                     x2[:].rearrange("p d g s -> p (d g) s"), [bx1, bx2],
                     Cx[:].rearrange("p d g i -> p (d g) i"), Cy[:].rearrange("p d g i -> p (d g) i"),
                     [bCx, bCy], 16, "q")
        with Phase(P) as p4:
            dps = [psr.get() for _ in range(4)]
            for d in range(2):
                for hf in range(2):
                    ps, bps = dps[d * 2 + hf]
                    for k in range(4):
                        gl = hf * 4 + k
                        gd = d * 8 + gl
                        P.mm(ps[:, k * 128:(k + 1) * 128], [(Pt0[:, gd, :], PQf[:, gd, 0:128])], [bPt0, bPQ], [bps],
                             strict=False)
            d1, bd1 = p4.sb("d1", [128, 8, 128], F32)
            d2, bd2 = p4.sb("d2", [128, 8, 128], F32)
            mf = bc(cst[:, OFF_MF:OFF_MF + 128].unsqueeze(1), [128, 4, 128])
            mb = bc(cst[:, OFF_MB:OFF_MB + 128].unsqueeze(1), [128, 4, 128])
            for hf in range(2):
                V("tensor_tensor", [dps[hf][1], bcst], [bd1], out=d1[:, hf * 4:(hf + 1) * 4],
                  in0=dps[hf][0][:].rearrange("p (k c) -> p k c", k=4), in1=mf, op=ALU.mult, strict=False)
                V("tensor_tensor", [dps[2 + hf][1], bcst], [bd2], out=d2[:, hf * 4:(hf + 1) * 4],
                  in0=dps[2 + hf][0][:].rearrange("p (k c) -> p k c", k=4), in1=mb, op=ALU.mult, strict=False)
            GP("tensor_tensor", [bd2], [bd1], out=d1[:], in0=d1[:], in1=d2[:], op=ALU.add)
            for gl in range(8):
                V("scalar_tensor_tensor", [bd1, bcst, S.bdcol], [bDsb], out=Dsb[:, gl], in0=identf,
                  scalar=S.dcol[:, g0 + gl:g0 + gl + 1], in1=d1[:, gl], op0=ALU.mult, op1=ALU.add, strict=False)
        with Phase(P) as p5:
            So, bSo = p5.sb("So", [128, 16, 8, 32], F32)
            Z, bZ = p5.sb("Z", [128, 16, 8, 32], BF16)
            for g2 in range(8):
                ps, bps = psr.get()
                for k in range(2):
                    gd = g2 * 2 + k
                    for j in range(8):
                        P.mm(ps[:, (k * 8 + j) * 32:(k * 8 + j + 1) * 32], [(Psb[:, gd, j, :], U[:, gd % 8, j, 0:32])],
                             [bPsb, bU], [bps], strict=False)
                evac(P, g2, [bps], [bSo], So[:, g2 * 2:(g2 + 1) * 2].rearrange("p a j r -> p (a j r)"), ps[:],
                     strict=False)
            for d in range(2):
                gsl = slice(d * 8, (d + 1) * 8)
                order = list(range(8)) if d == 0 else list(range(7, -1, -1))
                eng = V if d == 0 else GP
                eng("tensor_tensor", [bSo, bZ0p], [bSo], out=So[:, gsl, order[0]], in0=So[:, gsl, order[0]],
                    in1=Z0p[:, gsl], op=ALU.add)
                for a in range(1, 7):
                    eng("tensor_tensor", [bSo], [bSo], out=So[:, gsl, order[a]], in0=So[:, gsl, order[a]],
                        in1=So[:, gsl, order[a - 1]], op=ALU.add)
                eng("tensor_copy", [bZ0p], [bZ], out=Z[:, gsl, order[0]], in_=Z0p[:, gsl], strict=False)
                if d == 0:
                    eng("tensor_copy", [bSo], [bZ], out=Z[:, gsl, 1:8], in_=So[:, gsl, 0:7], strict=False)
                else:
                    eng("tensor_copy", [bSo], [bZ], out=Z[:, gsl, 0:7], in_=So[:, gsl, 1:8], strict=False)
            Yst, bYst = p5.sb("Yst", [128, 8, 8, 32], BF16)
            gtmp = Ring([p5.sb("gt", [128, 512], F32) for _ in range(4)])
            for g2 in range(4):
                ps, bps = psr.get()
                for k in range(2):
                    gl = g2 * 2 + k
                    for jt in range(8):
                        o = ps[:, (k * 8 + jt) * 32:(k * 8 + jt + 1) * 32]
                        P.mm(o, [(PQf[:, gl, jt * 128:(jt + 1) * 128], Z[:, gl, jt, :]),
                                 (PQf[:, 8 + gl, jt * 128:(jt + 1) * 128], Z[:, 8 + gl, jt, :]),
                                 (Dsb[:, gl, :], U[:, gl, jt, 0:32])], [bPQ, bZ, bDsb, bU], [bps], strict=False)
                x2_, bx2_ = gtmp.get()
                x3_, bx3_ = gtmp.get()
                ACT("activation", [bps], [bx2_], out=x2_[:], in_=ps[:], func=AF.Square)
                V("tensor_scalar", [bx2_], [bx2_], out=x2_[:], in0=x2_[:], scalar1=0.044715, scalar2=1.0,
                  op0=ALU.mult, op1=ALU.add)
                V("tensor_tensor", [bx2_, bps], [bx3_], out=x3_[:], in0=x2_[:], in1=ps[:], op=ALU.mult)
                ACT("activation", [bx3_], [bx3_], out=x3_[:], in_=x3_[:], func=AF.Sigmoid, scale=GELU_C)
                V("tensor_tensor", [bx3_, bps], [bYst], out=Yst[:, g2 * 2:(g2 + 1) * 2].rearrange("p a j r -> p (a j r)"),
                  in0=x3_[:], in1=ps[:], op=ALU.mult, strict=False)
            for tl in range(8):
                dst = bass.AP(T.Ys.tensor, gt * 128 * cfg.NK + tl * 32, [[8 * cfg.NK, 16], [256, 64], [1, 32]])
                P.dma("sync", dst, Yst[tl * 16:(tl + 1) * 16].rearrange("p g j r -> p (g j) r"), reads=[bYst],
                      writes=[tb(T, "Ys", gt)], strict=False)


def s5_pow(S, ph, prm, ev, n, tag):
    P = S.P
    shp = [128, 2, 8, n]
    ere, bre = ph.sb("ere" + tag, shp, F32)
    eim, bim = ph.sb("eim" + tag, shp, F32)
    with Phase(P) as pi:
        ang, b0 = pi.sb("ang" + tag, shp, F32)
        w1, b1 = pi.sb("pw1" + tag, shp, F32)
        w2, b2 = pi.sb("pw2" + tag, shp, F32)
        rd = [S.bpar, S.bcst]
        a3 = bc(prm(3).unsqueeze(3), shp)
        t3 = bc(prm(4).unsqueeze(3), shp)
        P.I("vector", "tensor_tensor", rd, [b0], out=ang[:], in0=t3, in1=ev, op=ALU.mult)
        P.I("gpsimd", "tensor_tensor", rd, [bre], out=ere[:], in0=a3, in1=ev, op=ALU.mult)
        P.I("scalar", "activation", [bre], [bre], out=ere[:], in_=ere[:], func=AF.Exp)
        for (shift, dst, bdst) in ((0.0, w1, b1), (0.25, w2, b2)):
            P.I("vector", "tensor_scalar", [b0], [bdst], out=dst[:], in0=ang[:], scalar1=1.0 / TWO_PI,
                scalar2=shift, op0=ALU.mult, op1=ALU.add)
            P.I("vector", "tensor_scalar_add", [bdst], [bim], out=eim[:], in0=dst[:], scalar1=MAGIC)
            P.I("vector", "tensor_scalar_add", [bim], [bim], out=eim[:], in0=eim[:], scalar1=-MAGIC)
            P.I("vector", "tensor_tensor", [bim], [bdst], out=dst[:], in0=dst[:], in1=eim[:], op=ALU.subtract)
            P.I("scalar", "activation", [bdst], [bdst], out=dst[:], in_=dst[:], func=AF.Sin, scale=TWO_PI)
        P.I("vector", "tensor_tensor", [bre, b1], [bim], out=eim[:], in0=ere[:], in1=w1[:], op=ALU.mult)
        P.I("gpsimd", "tensor_tensor", [b2], [bre], out=ere[:], in0=ere[:], in1=w2[:], op=ALU.mult)
    return (ere, bre), (eim, bim)


STAGES = [stage_adaln, stage_norm1, stage_win, stage_fourier_ab, stage_dft, stage_s5, stage_glu, stage_back,
          stage_final]


_CACHE = {}


def kernel(**inputs):
    cfg = Cfg()
    if "nc" not in _CACHE:
        _CACHE["nc"] = build(cfg, upto=99, dbg=0)
    nc = _CACHE["nc"]
    shared = None
    in_maps = []
    for core in range(8):
        m = make_in_map(cfg, inputs, core)
        if shared is None:
            shared = m
        else:
            for k in ("ada_w", "ada_b", "norm1_g", "norm2_g", "final_g", "w_in", "w_out", "fourier_w", "lam_re",
                      "lam_im", "log_dt", "b_re", "b_im", "c_re", "c_im", "s5_d", "glu_w_a", "glu_b_a", "glu_w_b",
                      "glu_b_b", "ffn_w_gate", "ffn_w_up", "ffn_w_down", "cdft", "identb"):
                m[k] = shared[k]
        in_maps.append(m)
    res = run_bass_kernel_spmd(nc, in_maps, core_ids=list(range(8)))
    D = cfg.D
    out = np.empty((2, 128, 64, D), np.float32)
    for core in range(8):
        b, q = core // 4, core % 4
        o = np.asarray(res.results[core]["out_own"], dtype=np.float32).reshape(64, 32, D)
        out[b, 32 * q:32 * q + 32] = o.transpose(1, 0, 2)
    return out.reshape(2, cfg.L, D)


def stage_front(P, cfg, T, per):
    D, KC = cfg.D, cfg.KC
    modT, bmodA = per["modT"]
    bmodB = per["bmodB"]
    gs, bgsA = per["gs"]
    bgsB = per["bgsB"]
    with Phase(P) as ph:
        cT, bcT = ph.sb("cT", [128, 2, KC], F32)
        for v in range(2):
            P.dma("sync", cT[:, v], T.cvec[v].rearrange("(kc p) -> p kc", p=128), writes=[bcT], strict=False,
                  allow_slow_non_contiguous=True)
        sT, bsT = ph.sb("sT", [128, KC, 2], F32)
        for v in range(2):
            P.I("scalar", "activation", [bcT], [bsT], out=sT[:, :, v], in_=cT[:, v, :], func=AF.Silu,
                strict=False)
        ng, bng = ph.sb("ng", [128, 2, KC], F32)
        P.dma("sync", ng[:, 0], T.norm1_g[0].rearrange("(kc p) -> p kc", p=128), writes=[bng], strict=False,
              allow_slow_non_contiguous=True)
        P.dma("sync", ng[:, 1], T.norm2_g[0].rearrange("(kc p) -> p kc", p=128), writes=[bng], strict=False,
              allow_slow_non_contiguous=True)
        KH = min(KC, 8)
        NKH = KC // KH
        wring = ph.ring("adaw", 4, [128, KH, 512], F32)
        psr = ph.ring("adaps", 2, [2, 512], F32, psum=True)
        bring = ph.ring("adab", 3, [2, 512], F32)
        oring = ph.ring("adao", 3, [2, 512], F32)
        wv = T.ada_w.rearrange("(kc p) n -> p kc n", p=128)
        NBLK = 6 * D // 512
        NEARLY = 2 * D // 512

        def ada_block(nb):
            ps, bps = psr.get()
            for kh in range(NKH):
                w, bw = wring.get()
                P.dma("gpsimd", w[:], wv[:, kh * KH:(kh + 1) * KH, nb * 512:(nb + 1) * 512], writes=[bw])
                P.mm(ps[:], [(sT[:, kh * KH + k, :], w[:, k, :]) for k in range(KH)], [bsT, bw], [bps],
                     start=(kh == 0), stop=(kh == NKH - 1))
            bt, bbt = bring.get()
            P.dma("scalar", bt[:], row_bcast(T.ada_b[0:1, nb * 512:(nb + 1) * 512], 2), writes=[bbt])
            o, bo = oring.get()
            P.I("vector", "tensor_tensor", [bps, bbt], [bo], out=o[:], in0=ps[:], in1=bt[:], op=ALU.add)
            key = "modscrA" if nb < NEARLY else "modscr"
            P.dma("scalar", T.modscr[:, nb * 512:(nb + 1) * 512], o[:], reads=[bo], writes=[tb(T, key)],
                  strict=False)

        def load_mod(ms, key, bm, bg_, gl):
            for v in range(2):
                for m_ in ms:
                    P.dma("sync", modT[:, v, m_], T.modscr[v, m_ * D:(m_ + 1) * D].rearrange("(kc p) -> p kc", p=128),
                          reads=[tb(T, key)], writes=[bm], strict=False, allow_slow_non_contiguous=True)
            for i, (v, m, n) in gl:
                P.I("vector", "scalar_tensor_tensor", [bm, bng], [bg_], out=gs[:, i], in0=modT[:, v, m],
                    scalar=1.0, in1=ng[:, n], op0=ALU.add, op1=ALU.mult, strict=False)
        for nb in range(NEARLY):
            ada_block(nb)
        load_mod((0, 1), "modscrA", bmodA, bgsA, [(0, (0, 1, 0)), (1, (1, 1, 0))])
        xr = ph.ring("x", 2, [128, D], F32)
        xnr = ph.ring("xn", 2, [128, D], BF16)
        junk = ph.sb("junk", [128, D], BF16)
        st = ph.ring("st", 4, [128, 4], F32)
        tps = ph.ring("tps", 4, [128, 8, 128], BF16, psum=True)
        hblk = ph.ring("hblk", 2, [128, KC, 512], BF16)
        xv = T.xb.rearrange("(r s) d -> s r d", s=64)
        nb_next = NEARLY
        tcount = 0
        for blk in range(cfg.NB + 1):
            lat = blk < cfg.NB
            nt = 4 if lat else 2
            hb, bhb = hblk.get()
            for ti in range(nt):
                x, bx = xr.get()
                src = xv[blk * 4 + ti] if lat else T.xctx[ti * 128:(ti + 1) * 128, :]
                P.dma("sync", x[:], src, writes=[bx])
                rstd, bst = norm_tile(P, ph, x[:], bx, D, st, junk)
                xn, bxn = xnr.get()
                P.I("vector", "tensor_scalar", [bx, bst], [bxn], out=xn[:], in0=x[:], scalar1=rstd,
                    scalar2=None, op0=ALU.mult)
                vi = 0 if lat else 1
                transpose_mod(P, xn, bxn, 0, KC, per["ident"], tps, hb, bhb, ti * 128, 128,
                              gs[:, vi], modT[:, vi, 0], [bmodA, bgsA], 0)
                tcount += 1
                if tcount % 2 == 0 and nb_next < NBLK:
                    ada_block(nb_next)
                    nb_next += 1
            if lat:
                P.dma("scalar", T.hTs[blk], hb[:], reads=[bhb], writes=[tb(T, "hTs", blk)])
            else:
                P.dma("scalar", T.hTc, hb[:, :, 0:256], reads=[bhb], writes=[tb(T, "hTc")])
        while nb_next < NBLK:
            ada_block(nb_next)
            nb_next += 1
        load_mod((2, 3, 4, 5), "modscr", bmodB, bgsB, [(2, (0, 4, 1))])


STAGES = [stage_front, stage_win, stage_fourier_ab, stage_dft, stage_s5, stage_glu, stage_back, stage_final]
PER_STAGES = {"stage_back", "stage_front"}


def dft_gen(P, cfg, T, ph):
    FCH = cfg.FCH
    CG = min(4, FCH)
    abr = ph.ring("ab", 3, [128, 2, CG * 128], BF16)
    tr = ph.ring("tab", 3, [128, 2, 512], BF16)
    psb = [ph.ps("dps", [128, 512]) for _ in range(CG)]
    yst = ph.ring("yst", 2, [128, 512], BF16)
    ne = 0
    for cg in range(FCH // CG):
        for kb in range(4):
            for n_ in range(64):
                a, ba = abr.get()
                t, bt = tr.get()
                P.dma("sync", a[:], T.ABs[n_][:, :, cg * CG * 128:(cg + 1) * CG * 128],
                      reads=[tb(T, "ABs", n_)], writes=[ba])
                P.dma("sync", t[:], T.tdft[kb, n_], writes=[bt])

                def fn(e, a=a, t=t, n_=n_):
                    ins = None
                    for ch in range(CG):
                        e.matmul(psb[ch][0][:], a[:, 0, ch * 128:(ch + 1) * 128], t[:, 0, :],
                                 start=(n_ == 0), stop=False)
                        ins = e.matmul(psb[ch][0][:], a[:, 1, ch * 128:(ch + 1) * 128], t[:, 1, :],
                                       start=False, stop=(n_ == 63))
                    return ins
                P.op("tensor", fn, [ba, bt], [psb[ch][1] for ch in range(CG)])
                yield
            for ch in range(CG):
                y, by = yst.get()
                ne += 1
                evac(P, ne, [psb[ch][1]], [by], y[:], psb[ch][0][:])
                P.dma("scalar", T.ycat[cg * CG + ch][:, kb * 512:(kb + 1) * 512], y[:], reads=[by],
                      writes=[tb(T, "ycat", (cg * CG + ch, kb))])
            yield


STAGES = [stage_front, stage_win, stage_fourier_ab, stage_dft, stage_s5, stage_glu, stage_back, stage_final]
```
